# Optimizing a Trainium2 kernel written in Bass

```python
import math
import jax, jax.numpy as jnp
from jax import lax
import numpy as np

D_MODEL = 1024
BATCH = 4
SEQ = 8192
DEPTH = 1

HEAD_DIM = 64
A_HEADS = D_MODEL // (2 * HEAD_DIM)
A_KV_HEADS = 2
A_GROUP = A_HEADS // A_KV_HEADS
A_WIDTH = A_HEADS * HEAD_DIM
A_KV_WIDTH = A_KV_HEADS * HEAD_DIM
WINDOW = 128
B_HEADS = D_MODEL // (2 * HEAD_DIM)
B_WIDTH = B_HEADS * HEAD_DIM
MOBA_BLOCK = 256
MOBA_TOPK = 3
MOBA_CHUNK = 64
N_BUCKETS = 32
MAX_DISTANCE = 128
DN_ALPHA = (2.0 * DEPTH) ** 0.25
DN_BETA = (8.0 * DEPTH) ** -0.25
LN_EPS = 1e-5
NEG = -1e30
ATTN_SCALE = HEAD_DIM ** -0.5

IN_SIZES = (A_WIDTH, A_KV_WIDTH, A_KV_WIDTH, A_WIDTH,
            B_WIDTH, B_WIDTH, B_WIDTH, B_WIDTH,
            D_MODEL, D_MODEL)
IN_TOTAL = sum(IN_SIZES)

kernel_name = "hybrid_swa_sink_moba_gated_deepnorm"


def _split_cols(h):
    parts, start = [], 0
    for size in IN_SIZES:
        parts.append(h[..., start:start + size])
        start += size
    return parts


def t5_bucket(dist):
    max_exact = N_BUCKETS // 2
    n = jnp.maximum(dist, 0)
    nf = jnp.maximum(n, 1).astype(jnp.float32)
    large = max_exact + (jnp.log(nf / max_exact) / math.log(MAX_DISTANCE / max_exact)
                         * (N_BUCKETS - max_exact)).astype(jnp.int32)
    large = jnp.minimum(large, N_BUCKETS - 1)
    return jnp.where(n < max_exact, n, large)


def layer_norm(x, g, b):
    xf = x.astype(jnp.float32)
    mu = jnp.mean(xf, axis=-1, keepdims=True)
    var = jnp.mean(jnp.square(xf - mu), axis=-1, keepdims=True)
    return ((xf - mu) * lax.rsqrt(var + LN_EPS) * g.astype(jnp.float32)
            + b.astype(jnp.float32)).astype(x.dtype)


def swa_sink_attention(q, k, v, sinks, rel_table):
    B_, S_, _ = q.shape
    nb = S_ // WINDOW
    qb = q.reshape(B_, nb, WINDOW, A_KV_HEADS, A_GROUP, HEAD_DIM)
    kb = k.reshape(B_, nb, WINDOW, A_KV_HEADS, HEAD_DIM)
    vb = v.reshape(B_, nb, WINDOW, A_KV_HEADS, HEAD_DIM)
    pad = jnp.zeros_like(kb[:, :1])
    k2 = jnp.concatenate([jnp.concatenate([pad, kb[:, :-1]], axis=1), kb], axis=2)
    v2 = jnp.concatenate([jnp.concatenate([pad, vb[:, :-1]], axis=1), vb], axis=2)
    logits = jnp.einsum('bnqkgd,bnskd->bnkgqs', qb, k2,
                        preferred_element_type=jnp.float32) * ATTN_SCALE
    i = jnp.arange(WINDOW)[:, None]
    j = jnp.arange(2 * WINDOW)[None, :]
    dist = WINDOW + i - j
    bias = rel_table.astype(jnp.float32)[t5_bucket(dist)]
    bias = bias.transpose(2, 0, 1).reshape(A_KV_HEADS, A_GROUP, WINDOW, 2 * WINDOW)
    valid = (dist >= 0) & (dist < WINDOW)
    first = (jnp.arange(nb) == 0)[:, None, None] & (j < WINDOW)[None]
    mask = valid[None] & ~first
    logits = jnp.where(mask[None, :, None, None], logits + bias[None, None], NEG)
    sink = jnp.broadcast_to(sinks.astype(jnp.float32).reshape(A_KV_HEADS, A_GROUP)[None, None, :, :, None, None],
                            logits.shape[:-1] + (1,))
    p = jax.nn.softmax(jnp.concatenate([logits, sink], axis=-1), axis=-1)[..., :-1]
    o = jnp.einsum('bnkgqs,bnskd->bnqkgd', p.astype(v.dtype), v2)
    return o.reshape(B_, S_, A_WIDTH)


def moba_attention(q, k, v, rel_table):
    B_, S_, _ = q.shape
    s_pad = -(-S_ // MOBA_BLOCK) * MOBA_BLOCK
    padw = ((0, 0), (0, s_pad - S_), (0, 0))
    q, k, v = [jnp.pad(t, padw).reshape(B_, s_pad, B_HEADS, HEAD_DIM).transpose(0, 2, 1, 3)
               for t in (q, k, v)]
    nblk = s_pad // MOBA_BLOCK
    k_eff = min(MOBA_TOPK, nblk)
    kblk = k.reshape(B_, B_HEADS, nblk, MOBA_BLOCK, HEAD_DIM)
    vblk = v.reshape(B_, B_HEADS, nblk, MOBA_BLOCK, HEAD_DIM)
    kmean = jnp.mean(kblk.astype(jnp.float32), axis=3)
    gate = jnp.einsum('bhtd,bhnd->bhtn', q.astype(jnp.float32), kmean)
    q_blk = jnp.arange(s_pad) // MOBA_BLOCK
    past = jnp.arange(nblk)[None, :] < q_blk[:, None]
    gate = jnp.where(past, gate, NEG)
    _, sel = lax.top_k(gate, k_eff)
    n_chunks = s_pad // MOBA_CHUNK
    qc = q.reshape(B_, B_HEADS, n_chunks, MOBA_CHUNK, HEAD_DIM).transpose(2, 0, 1, 3, 4)
    selc = sel.reshape(B_, B_HEADS, n_chunks, MOBA_CHUNK, k_eff).transpose(2, 0, 1, 3, 4)
    bi = jnp.arange(B_)[:, None, None, None]
    hi = jnp.arange(B_HEADS)[None, :, None, None]
    hi5 = jnp.arange(B_HEADS)[None, :, None, None, None]
    table_t = rel_table.astype(jnp.float32).T
    blk_ar = jnp.arange(MOBA_BLOCK)

    def chunk_fn(args):
        c, q_c, sel_c = args
        t = c * MOBA_CHUNK + jnp.arange(MOBA_CHUNK)
        own = t[0] // MOBA_BLOCK
        k_own = lax.dynamic_index_in_dim(kblk, own, axis=2, keepdims=False)
        v_own = lax.dynamic_index_in_dim(vblk, own, axis=2, keepdims=False)
        k_sel = kblk[bi, hi, sel_c]
        v_sel = vblk[bi, hi, sel_c]
        d_own = t[:, None] - (own * MOBA_BLOCK + blk_ar)[None, :]
        l_own = (jnp.einsum('bhcd,bhsd->bhcs', q_c, k_own, preferred_element_type=jnp.float32) * ATTN_SCALE
                 + table_t[:, t5_bucket(d_own)][None])
        l_own = jnp.where(d_own >= 0, l_own, NEG)
        d_sel = t[None, None, :, None, None] - (sel_c[..., None] * MOBA_BLOCK + blk_ar)
        l_sel = (jnp.einsum('bhcd,bhcksd->bhcks', q_c, k_sel, preferred_element_type=jnp.float32) * ATTN_SCALE
                 + table_t[hi5, t5_bucket(d_sel)])
        valid_sel = jnp.arange(k_eff)[None, :] < (t // MOBA_BLOCK)[:, None]
        l_sel = jnp.where(valid_sel[:, :, None], l_sel, NEG)
        B2, H2, C2 = q_c.shape[:3]
        logits = jnp.concatenate([l_own, l_sel.reshape(B2, H2, C2, k_eff * MOBA_BLOCK)], axis=-1)
        p = jax.nn.softmax(logits, axis=-1).astype(v.dtype)
        p_own = p[..., :MOBA_BLOCK]
        p_sel = p[..., MOBA_BLOCK:].reshape(B2, H2, C2, k_eff, MOBA_BLOCK)
        return (jnp.einsum('bhcs,bhsd->bhcd', p_own, v_own)
                + jnp.einsum('bhcks,bhcksd->bhcd', p_sel, v_sel))

    out = lax.map(chunk_fn, (jnp.arange(n_chunks), qc, selc))
    out = out.transpose(1, 0, 3, 2, 4).reshape(B_, s_pad, B_WIDTH)
    return out[:, :S_]


def setup_inputs(seed: int = 0) -> dict:
    key = jax.random.key(seed)
    ks = jax.random.split(key, 20)
    x = jax.random.normal(ks[0], (BATCH, SEQ, D_MODEL), jnp.float32)
    col_scale = (1.0, 1.0, DN_BETA, 1.0, 1.0, 1.0, DN_BETA, 1.0, 1.0, 1.0)
    blocks = [jax.random.normal(ks[1 + i], (DEPTH, D_MODEL, n), jnp.float32) * (D_MODEL ** -0.5 * s)
              for i, (n, s) in enumerate(zip(IN_SIZES, col_scale))]
    w_in = jnp.concatenate(blocks, axis=-1)
    b_gate = 0.1 * jax.random.normal(ks[11], (DEPTH, 2 * D_MODEL), jnp.float32)
    sinks = 0.5 * jax.random.normal(ks[12], (DEPTH, A_HEADS), jnp.float32)
    rel_table = 0.5 * jax.random.normal(ks[13], (N_BUCKETS, A_HEADS + B_HEADS), jnp.float32)
    w_out_a = jax.random.normal(ks[14], (DEPTH, A_WIDTH, D_MODEL), jnp.float32) * (A_WIDTH ** -0.5 * DN_BETA)
    w_out_b = jax.random.normal(ks[15], (DEPTH, B_WIDTH, D_MODEL), jnp.float32) * (B_WIDTH ** -0.5 * DN_BETA)
    w_out = jax.random.normal(ks[16], (DEPTH, D_MODEL, D_MODEL), jnp.float32) * (D_MODEL ** -0.5 * DN_BETA)
    ln_gamma = 1.0 + 0.01 * jax.random.normal(ks[17], (DEPTH, D_MODEL), jnp.float32)
    ln_beta = 0.01 * jax.random.normal(ks[18], (DEPTH, D_MODEL), jnp.float32)
    return {"x": x, "w_in": w_in, "b_gate": b_gate, "sinks": sinks, "rel_table": rel_table,
            "w_out_a": w_out_a, "w_out_b": w_out_b, "w_out": w_out,
            "ln_gamma": ln_gamma, "ln_beta": ln_beta}


def reference(x, w_in, b_gate, sinks, rel_table, w_out_a, w_out_b, w_out, ln_gamma, ln_beta):
    table_a = rel_table[:, :A_HEADS]
    table_b = rel_table[:, A_HEADS:]
    for l in range(DEPTH):
        h = jnp.einsum('bsd,de->bse', x, w_in[l])
        qa, ka, va, za, qb, kb, vb, zb, ga, gb = _split_cols(h)
        ya = swa_sink_attention(qa, ka, va, sinks[l], table_a) * jax.nn.silu(za)
        yb = moba_attention(qb, kb, vb, table_b) * jax.nn.silu(zb)
        ya = jnp.einsum('bse,ed->bsd', ya, w_out_a[l])
        yb = jnp.einsum('bse,ed->bsd', yb, w_out_b[l])
        gate_a = jax.nn.sigmoid(ga + b_gate[l, :D_MODEL])
        gate_b = jax.nn.sigmoid(gb + b_gate[l, D_MODEL:])
        merged = gate_a * ya + gate_b * yb
        out = jnp.einsum('bsd,de->bse', merged, w_out[l])
        x = layer_norm(DN_ALPHA * x + out, ln_gamma[l], ln_beta[l])
    return x
```

```python
import math
from contextlib import ExitStack

import numpy as np
import concourse.bass as bass
import concourse.mybir as mybir
from concourse.bass_utils import run_bass_kernel_spmd

F32 = mybir.dt.float32
BF16 = mybir.dt.bfloat16
AF = mybir.ActivationFunctionType
ALU = mybir.AluOpType
AX = mybir.AxisListType

D = 1024
SEQ = 8192
NSLOT = 16
NCORES = 8
ALPHA = 2.0 ** 0.25
LN_EPS = 1e-5
NEGM = -30000.0
FL = 768


class DSem:
    def __init__(self, prog, name):
        self.h = prog.stack.enter_context(prog.nc.semaphore(name))
        self.count = 0


class _Dummy:
    def then_inc(self, *a):
        pass


class Prog:
    ENG = ("pe", "act", "dve", "pool", "sp")

    def __init__(self, nc):
        self.nc = nc
        self.stack = ExitStack()
        self.ops = {e: [] for e in self.ENG}
        self.psem = {e: self.stack.enter_context(nc.semaphore("prog_" + e)) for e in self.ENG}
        self.pcount = {e: 0 for e in self.ENG}
        self.nd = 0
        self.all_dsems = []

    def dsem(self, name=None):
        self.nd += 1
        d = DSem(self, name or f"dsem{self.nd}")
        self.all_dsems.append(d)
        return d

    def sb(self, name, shape, dt):
        return self.stack.enter_context(self.nc.sbuf_tensor(name, shape, dt))

    def ps(self, name, shape, dt):
        return self.stack.enter_context(self.nc.psum_tensor(name, shape, dt))

    def op(self, eng, fn, waits=(), inc=True):
        tok = None
        if inc:
            self.pcount[eng] += 1
            tok = (self.psem[eng], self.pcount[eng])
        self.ops[eng].append((fn, [w for w in waits if w is not None], (self.psem[eng], 1) if inc else None))
        return tok

    def dma(self, eng, fn, dsem, waits=()):
        dsem.count += 16
        tok = (dsem.h, dsem.count)
        self.ops[eng].append((fn, [w for w in waits if w is not None], (dsem.h, 16)))
        return tok

    def wait(self, eng, waits):
        self.ops[eng].append((None, [w for w in waits if w is not None], None))

    def emit(self):
        nc = self.nc
        with nc.Block() as block:
            def replay(engname):
                def run(e):
                    waited = {}
                    for fn, waits, inc in self.ops[engname]:
                        for (s, v) in waits:
                            k = id(s)
                            if waited.get(k, 0) >= v:
                                continue
                            waited[k] = v
                            e.wait_ge(s, v)
                        if fn is None:
                            continue
                        ins = fn(e)
                        if inc is not None:
                            ins.then_inc(inc[0], inc[1])
                return run
            block.tensor(replay("pe"))
            block.scalar(replay("act"))
            block.vector(replay("dve"))
            block.gpsimd(replay("pool"))
            block.sync(replay("sp"))

    def close(self):
        self.stack.close()


def MM(P, out, lhsT, rhs, start, stop, waits=(), inc=False):
    return P.op("pe", lambda e: e.matmul(out, lhsT=lhsT, rhs=rhs, start=start, stop=stop,
                                         skip_group_check=True), waits=waits, inc=inc)


def ACTF(P, out, in_, func, waits=(), bias=None, scale=1.0, inc=True):
    if bias is None:
        return P.op("act", lambda e: e.activation(out=out, in_=in_, func=func, scale=scale), waits=waits, inc=inc)
    return P.op("act", lambda e: e.activation(out=out, in_=in_, func=func, bias=bias, scale=scale),
                waits=waits, inc=inc)


def COPY(P, eng, out, in_, waits=(), inc=True):
    if eng == "act":
        return P.op("act", lambda e: e.copy(out=out, in_=in_), waits=waits, inc=inc)
    return P.op(eng, lambda e: e.tensor_copy(out=out, in_=in_), waits=waits, inc=inc)


def TT(P, eng, out, in0, in1, op, waits=(), inc=True):
    return P.op(eng, lambda e: e.tensor_tensor(out=out, in0=in0, in1=in1, op=op), waits=waits, inc=inc)


def TS(P, eng, out, in0, s1, s2, op0, op1=None, waits=(), inc=True):
    if op1 is None:
        return P.op(eng, lambda e: e.tensor_scalar(out=out, in0=in0, scalar1=s1, scalar2=None, op0=op0),
                    waits=waits, inc=inc)
    return P.op(eng, lambda e: e.tensor_scalar(out=out, in0=in0, scalar1=s1, scalar2=s2, op0=op0, op1=op1),
                waits=waits, inc=inc)


def STT(P, out, in0, scalar, in1, op0, op1, waits=(), inc=True):
    return P.op("dve", lambda e: e.scalar_tensor_tensor(out=out, in0=in0, scalar=scalar, in1=in1, op0=op0, op1=op1),
                waits=waits, inc=inc)


def DMA(P, eng, out, in_, dsem, waits=()):
    return P.dma(eng, lambda e: e.dma_start(out=out, in_=in_), dsem, waits=waits)


def build_nc(stage=99):
    nc = bass.Bass("TRN2", target_bir_lowering=False)
    dt_in = lambda n, s, d=F32: nc.dram_tensor(n, s, d, kind="ExternalInput")
    xT_d = dt_in("xT", [D, SEQ])
    xown_d = dt_in("xown", [NSLOT * 256, D])
    wkv_d = dt_in("w_kv", [D, 1280])
    wown_d = dt_in("w_own", [D, 2048])
    wg_d = dt_in("w_g", [16, 128, 8, 128])
    woa_d = dt_in("w_oa", [8, 128, 4, 128])
    wob_d = dt_in("w_ob", [8, 128, 4, 128])
    wo_d = dt_in("w_o", [D, D])
    bg_d = dt_in("bg", [128, 16])
    rel_d = dt_in("rel", [32, 16])
    sinks_d = dt_in("sinks", [1, 8])
    lng_d = dt_in("lng", [1, D])
    lnb_d = dt_in("lnb", [1, D])
    ohA_d = dt_in("ohA", [32, FL])
    ohB_d = dt_in("ohB", [32, FL])
    mkA_d = dt_in("mkA", [8, FL])
    mkB_d = dt_in("mkB", [8, FL])
    pastneg_d = dt_in("pastneg", [128, 512])
    ownsel_d = dt_in("ownsel", [128, 512])
    firstneg_d = dt_in("firstneg", [128, 1])
    ident_d = dt_in("ident", [128, 128])
    antiid_d = dt_in("antiid", [128, 128])
    blkind_d = dt_in("blkind", [32, SEQ])
    out_d = nc.dram_tensor("out", [NSLOT * 256, D], F32, kind="ExternalOutput")
    kb_s = nc.dram_tensor("kb_s", [512, SEQ], BF16, kind="Internal")
    vb_s = nc.dram_tensor("vb_s", [8, 128, 64, 65], BF16, kind="Internal")
    qb_s = nc.dram_tensor("qb_s", [512, NSLOT * 256], BF16, kind="Internal")
    fv_s = nc.dram_tensor("fv_s", [16, FL], BF16, kind="Internal")

    P = Prog(nc)
    ZA = P.sb("ZA", [128, 16384], BF16)
    ZB = P.sb("ZB", [128, 16384], BF16)
    R1 = P.sb("R1", [128, 33792], BF16)
    R2 = P.sb("R2", [128, 20480], BF16)
    XT = P.sb("XT", [128, 8192], BF16)
    QST = P.sb("QST", [128, 2048], BF16)
    ident = P.sb("ident_bf", [128, 128], BF16)
    antiid = P.sb("antiid_bf", [128, 128], BF16)
    ones_f = P.sb("ones_f", [128, 128], F32)
    rel_sb = P.sb("rel_sb", [32, 16], F32)
    bg_sb = P.sb("bg_sb", [128, 16], F32)
    negbg = P.sb("negbg", [128, 16], F32)
    sink_sb = P.sb("sink_sb", [128, 8], F32)
    es_sb = P.sb("es_sb", [128, 8], F32)
    firstneg = P.sb("firstneg_sb", [128, 1], F32)
    ksum = P.sb("ksum", [128, 4 * 32], F32)
    kmean = P.sb("kmean", [64, 8 * 32], BF16)
    lnst = P.sb("lnst", [128, 2 * 12], F32)
    lnmv = P.sb("lnmv", [128, 12], F32)
    YTB = P.sb("YTB", [128, 2048], F32)
    PSB = [P.ps(f"psb{i}", [128, 1024], F32) for i in range(4)]
    PS = [PSB[i // 2][:, (i % 2) * 512:(i % 2 + 1) * 512] for i in range(8)]
    half_c = P.sb("half_c", [128, 1], F32)
    halfs_bf = P.sb("halfs_bf", [128, 128], BF16)
    hbg = P.sb("hbg", [128, 16], F32)

    def v3(ap, **kw):
        pat = kw.pop("pat")
        return ap.rearrange(pat, **kw)

    Wkv = v3(ZA[:, 0:10240], pat="p (k n) -> p k n", k=8)
    kst = [v3(ZB[:, b * 2048:(b + 1) * 2048], pat="p (a t) -> p a t", a=4) for b in range(2)]
    vst = [v3(ZB[:, 4096 + b * 2080:4096 + (b + 1) * 2080], pat="p (h t c) -> p h t c", h=8, t=4) for b in range(2)]
    KA = R1[:, 0:8192]
    VA = v3(R1[:, 8192:8192 + 8320], pat="p (t g c) -> p t g c", g=2, c=65)
    QA = v3(R1[:, 16512:16512 + 16384], pat="p (a t) -> p a t", a=4)
    Wown = v3(R2[:, 0:16384], pat="p (k n) -> p k n", k=8)
    etmp = [R2[:, 16384 + b * 1024:16384 + (b + 1) * 1024].bitcast(F32) for b in range(2)]
    xt = [v3(XT[:, b * 4096:(b + 1) * 4096], pat="p (k t) -> p k t", k=8) for b in range(2)]
    xq = [v3(XT[:, b * 4096:b * 4096 + 2048], pat="p (k t) -> p k t", k=8) for b in range(2)]
    qst = [v3(QST[:, b * 1024:(b + 1) * 1024], pat="p (a t) -> p a t", a=4) for b in range(2)]
    ZAv = v3(ZA[:, :], pat="p (a t) -> p a t", a=4)
    ZBv = v3(ZB[:, :], pat="p (a t) -> p a t", a=4)
    S0 = 16512
    ohA_sb = R1[0:32, S0:S0 + 2 * FL].bitcast(F32)
    ohB_sb = R1[0:32, S0 + 2 * FL:S0 + 4 * FL].bitcast(F32)
    mkA_sb = R1[0:8, S0 + 4 * FL:S0 + 6 * FL].bitcast(F32)
    mkB_sb = R1[0:8, S0 + 6 * FL:S0 + 8 * FL].bitcast(F32)
    fsbA = R1[0:8, S0 + 8 * FL:S0 + 9 * FL]
    fsbB = R1[0:8, S0 + 9 * FL:S0 + 10 * FL]
    Htab = v3(R2[:, 0:12288], pat="p (n q) -> p n q", q=256)
    PT = [R2[:, 12288 + b * 1024:12288 + (b + 1) * 1024] for b in range(3)]
    wtmp = [R2[:, 15360 + b * 1024:15360 + (b + 1) * 1024].bitcast(F32) for b in range(2)]
    otmp = [R2[:, 17408 + b * 1024:17408 + (b + 1) * 1024].bitcast(F32) for b in range(2)]
    rd = [QST[:, b * 1024:(b + 1) * 1024].bitcast(F32) for b in range(2)]
    pastneg = v3(XT[:, 0:1024].bitcast(F32), pat="p (s v) -> p s v", v=32)
    ownsel = v3(XT[:, 1024:2048].bitcast(F32), pat="p (s v) -> p s v", v=32)
    gm = v3(XT[:, 2048:4096].bitcast(F32), pat="p (t v) -> p t v", v=32)
    sel = v3(XT[:, 4096:6144].bitcast(F32), pat="p (t v) -> p t v", v=32)
    nm = v3(XT[:, 6144:7168], pat="p (t v) -> p t v", v=32)
    top8 = v3(XT[:, 7168:7680].bitcast(F32), pat="p (t e) -> p t e", e=8)
    thr = XT[:, 7680:7744].bitcast(F32)
    KB = [R1[:, b * 8192:(b + 1) * 8192] for b in range(2)]
    VB = [v3(R1[:, 16384 + b * 4160:16384 + (b + 1) * 4160], pat="p (t c) -> p t c", c=65) for b in range(2)]
    QB = [R1[:, 24704 + b * 4096:24704 + (b + 1) * 4096] for b in range(2)]
    Wg = v3(R1[:, 0:16384], pat="p (k n) -> p k n", k=8)
    Woa = v3(R1[:, 16384:20480], pat="p (k n) -> p k n", k=4)
    Wob = v3(R1[:, 20480:24576], pat="p (k n) -> p k n", k=4)
    Wo = v3(R1[:, 24576:32768], pat="p (k n) -> p k n", k=8)
    MG = v3(R2[:, 0:4096], pat="p (j t) -> p j t", j=8)
    tmp4 = [R2[:, 4096 + k * 1024:4096 + (k + 1) * 1024].bitcast(F32) for k in range(4)]
    xr = [R2[:, 8192 + b * 2048:8192 + (b + 1) * 2048].bitcast(F32) for b in range(2)]
    yt = [R2[:, 12288 + b * 2048:12288 + (b + 1) * 2048].bitcast(F32) for b in range(2)]
    YT = [yt[0], yt[1], YTB[:, 0:1024], YTB[:, 1024:2048]]
    rdh = [YTB[:, b * 256:(b + 1) * 256].bitcast(BF16) for b in range(2)]
    rdl = [YTB[:, 512 + b * 256:512 + (b + 1) * 256].bitcast(BF16) for b in range(2)]
    gam = R2[:, 16384:18432].bitcast(F32)
    bet = R2[:, 18432:20480].bitcast(F32)

    xT_v = xT_d.ap().rearrange("(k p) t -> p k t", p=128)

    bank_free = [None] * 8

    def _finish():
        toks = [(P.psem[e], P.pcount[e]) for e in ("pe", "act", "dve", "pool") if P.pcount[e] > 0]
        toks += [(ds.h, ds.count) for ds in P.all_dsems if ds.count > 0]
        P.wait("sp", toks)
        P.emit()
        P.close()
        return nc

    ds_c = P.dsem("ds_const")
    t_c = None
    for (o, i_) in ((rel_sb[:, :], rel_d.ap()), (ohA_sb, ohA_d.ap()), (ohB_sb, ohB_d.ap()),
                    (mkA_sb, mkA_d.ap()), (mkB_sb, mkB_d.ap()), (bg_sb[:, :], bg_d.ap()),
                    (sink_sb[64:65, :], sinks_d.ap()), (firstneg[:, :], firstneg_d.ap())):
        t_c = DMA(P, "sp", o, i_, ds_c)
    ds_cb = P.dsem("ds_const_bf")
    DMA(P, "pool", ident[:, :], ident_d.ap(), ds_cb)
    t_cb = DMA(P, "pool", antiid[:, :], antiid_d.ap(), ds_cb)
    ds_w1 = P.dsem("ds_wkv")
    t_wkv = DMA(P, "pool", Wkv, wkv_d.ap().rearrange("(k p) n -> p k n", p=128), ds_w1)
    t_m1 = P.op("pool", lambda e: e.memset(ones_f[:, :], 0.5))
    P.op("pool", lambda e: e.memset(half_c[:, :], 0.5))
    t_hb = P.op("pool", lambda e: e.memset(halfs_bf[:, :], 0.5))
    P.op("pool", lambda e: e.memset(VA[:, :, :, 64:65], 1.0))
    P.op("pool", lambda e: e.memset(vst[0][:, :, :, 64:65], 1.0))
    P.op("pool", lambda e: e.memset(vst[1][:, :, :, 64:65], 1.0))
    t_mset = P.op("pool", lambda e: e.memset(ksum[:, :], 0.0))
    ds_w2 = P.dsem("ds_wown")
    t_nb = TS(P, "dve", hbg[:, :], bg_sb[:, :], 0.5, None, ALU.mult, waits=[t_c])
    t_es = ACTF(P, es_sb[64:65, :], sink_sb[64:65, :], AF.Exp, waits=[t_c])
    t_fv = None
    ds_fv = P.dsem("ds_fv")
    for (oh, mk, fsb, c0, b0, r0) in ((ohA_sb, mkA_sb, fsbA, 0, 4, 0), (ohB_sb, mkB_sb, fsbB, 8, 6, 8)):
        t_st = None
        for hf in range(2):
            tmm = MM(P, PS[b0 + hf][0:8, 0:384], rel_sb[0:32, c0:c0 + 8], oh[:, hf * 384:(hf + 1) * 384],
                     True, True, waits=[t_c], inc=True)
            t_st = STT(P, fsb[:, hf * 384:(hf + 1) * 384], PS[b0 + hf][0:8, 0:384], 8.0,
                       mk[:, hf * 384:(hf + 1) * 384], ALU.mult, ALU.add, waits=[tmm])
            bank_free[b0 + hf] = t_st
        t_fv = DMA(P, "sp", fv_s.ap()[r0:r0 + 8, :], fsb, ds_fv, waits=[t_st])
    t_setup_done = t_fv
    if stage == 0:
        return _finish()

    ds_xt = [P.dsem("ds_xt0"), P.dsem("ds_xt1")]
    ds_ksp = [P.dsem("ds_ksp0"), P.dsem("ds_ksp1")]
    ds_vsp = [P.dsem("ds_vsp0"), P.dsem("ds_vsp1")]
    pe_chunk_done = {}
    t_ksp = {}
    t_vsp = {}
    kb_v = kb_s.ap().rearrange("(a r) t -> r a t", r=128)
    vb_v = vb_s.ap().rearrange("h p t c -> p h t c")
    kcnt = 0
    vcnt = 0
    for c in range(16):
        b = c % 2
        t_x = DMA(P, "pool", xt[b], xT_v[:, :, c * 512:(c + 1) * 512], ds_xt[b],
                  waits=[pe_chunk_done.get(c - 2)])
        if c == 1:
            t_wown = DMA(P, "pool", Wown, wown_d.ap().rearrange("(k p) n -> p k n", p=128), ds_w2)
        t_lastK = None
        for g in range(5):
            bk = kcnt % 2
            kcnt += 1
            for kc in range(8):
                tmm = MM(P, PS[bk][:, :], Wkv[:, kc, g * 128:(g + 1) * 128], xt[b][:, kc, :], kc == 0, kc == 7,
                         waits=[t_x, t_wkv, bank_free[bk]] if kc == 0 else [], inc=(kc == 7))
            if g == 0:
                te = COPY(P, "dve", KA[:, c * 512:(c + 1) * 512], PS[bk][:, :], waits=[tmm, t_setup_done])
            else:
                COPY(P, "dve", kst[b][:, g - 1, :], PS[bk][:, :], waits=[tmm, t_ksp.get(c - 2)], inc=False)
                te = P.op("dve", lambda e, o=ksum[:, (g - 1) * 32 + 2 * c:(g - 1) * 32 + 2 * c + 2],
                          i=PS[bk][:, :].rearrange("p (a t) -> p a t", a=2):
                          e.tensor_reduce(out=o, in_=i, axis=AX.X, op=ALU.add), waits=[t_mset])
                t_lastK = te
            bank_free[bk] = te
        t_ksp[c] = DMA(P, "sp", kb_v[:, :, c * 512:(c + 1) * 512], kst[b], ds_ksp[b], waits=[t_lastK])
        t_lastV = None
        for t in range(4):
            bv = 2 + vcnt % 2
            ba = 4 + vcnt % 2
            vcnt += 1
            for kc in range(8):
                tmv = MM(P, PS[bv][:, :], xt[b][:, kc, t * 128:(t + 1) * 128], Wkv[:, kc, 768:1280], kc == 0, kc == 7,
                         waits=[bank_free[bv]] if kc == 0 else [], inc=(kc == 7))
            for kc in range(8):
                tma = MM(P, PS[ba][:, 0:128], xt[b][:, kc, t * 128:(t + 1) * 128], Wkv[:, kc, 640:768], kc == 0, kc == 7,
                         waits=[bank_free[ba]] if kc == 0 else [], inc=(kc == 7))
            te = COPY(P, "act", vst[b][:, :, t, 0:64], PS[bv][:, :].rearrange("p (h d) -> p h d", h=8),
                      waits=[tmv, t_vsp.get(c - 2)])
            bank_free[bv] = te
            te = COPY(P, "act", VA[:, c * 4 + t, :, 0:64], PS[ba][:, 0:128].rearrange("p (g d) -> p g d", g=2),
                      waits=[tma, t_setup_done])
            bank_free[ba] = te
            t_lastV = te
        pe_chunk_done[c] = tma
        t_vsp[c] = DMA(P, "sp", vb_v[:, :, c * 4:(c + 1) * 4, :], vst[b], ds_vsp[b], waits=[t_lastV])
    t_p1_pe = pe_chunk_done[15]
    kmv = kmean[:, :].rearrange("p (h v) -> p h v", v=32)
    ksv = ksum[:, :].rearrange("p (a v) -> p a v", v=32)
    kmv4 = kmean[:, :].rearrange("p (a two v) -> p a two v", two=2, v=32)
    TS(P, "dve", kmv4[0:64, :, 0, :], ksv[0:64, :, :], 1.0 / 256.0, None, ALU.mult, waits=[t_lastK])
    t_kmean = TS(P, "dve", kmv4[0:64, :, 1, :], ksv[64:128, :, :], 1.0 / 256.0, None, ALU.mult, waits=[t_lastK])

    if stage == 1:
        return _finish()

    ds_xq = [P.dsem("ds_xq0"), P.dsem("ds_xq1")]
    ds_qsp = [P.dsem("ds_qsp0"), P.dsem("ds_qsp1")]
    qb_v = qb_s.ap().rearrange("(a r) t -> r a t", r=128)
    pe_slot_done = {}
    t_qsp = {}
    t_e_free = [None, None]
    ecnt = 0
    bcnt = 0
    t_p1_spill = [t_ksp[14], t_ksp[15], t_vsp[14], t_vsp[15]]
    for i in range(NSLOT):
        b = i % 2
        w = [pe_slot_done.get(i - 2)] if i >= 2 else [pe_chunk_done[14 + b]]
        t_x = DMA(P, "pool", xq[b], xT_v[:, :, (2 * i + 1) * 256:(2 * i + 2) * 256], ds_xq[b], waits=w)
        tsl = slice(i * 256, (i + 1) * 256)
        t_lastq = None
        for bi in range(8):
            bk = bcnt % 4
            bcnt += 1
            for jj in range(2):
                j = 2 * bi + jj
                for kc in range(8):
                    tmm = MM(P, PS[bk][:, jj * 256:(jj + 1) * 256], Wown[:, kc, j * 128:(j + 1) * 128], xq[b][:, kc, :],
                             kc == 0, kc == 7,
                             waits=[t_x, t_wown, bank_free[bk]] if (kc == 0 and jj == 0) else [],
                             inc=(kc == 7 and jj == 1))
            pv = PS[bk][:, :].rearrange("p (a t) -> p a t", a=2)
            kind = bi // 2
            a0 = 2 * (bi % 2)
            if kind == 0:
                te = COPY(P, "dve", QA[:, a0:a0 + 2, tsl], pv, waits=[tmm, t_setup_done])
            elif kind == 2:
                te = COPY(P, "dve", qst[b][:, a0:a0 + 2, :], pv, waits=[tmm, t_qsp.get(i - 2)])
                t_lastq = te
            else:
                eb = ecnt % 2
                ecnt += 1
                Z = ZAv if kind == 1 else ZBv
                t3 = ACTF(P, etmp[eb], PS[bk][:, :], AF.Tanh, scale=0.5, waits=[tmm, t_e_free[eb]])
                te = STT(P, Z[:, a0:a0 + 2, tsl], etmp[eb].rearrange("p (a t) -> p a t", a=2), 1.0, pv, ALU.add, ALU.mult,
                         waits=[t3, t_p1_pe] + t_p1_spill)
                t_e_free[eb] = te
            bank_free[bk] = te
        pe_slot_done[i] = tmm
        t_qsp[i] = DMA(P, "sp", qb_v[:, :, tsl], qst[b], ds_qsp[b], waits=[t_lastq])
    t_p2_pe = pe_slot_done[NSLOT - 1]
    t_p2_last_evac = bank_free[(bcnt - 1) % 4]
    if stage == 2:
        return _finish()

    ds_h = P.dsem("ds_htab")
    t_h = None
    for mix in range(2):
        for h in range(8):
            for kind, off in ((0, 128), (1, 0), (2, 256)):
                src = bass.AP(fv_s, (mix * 8 + h) * FL + off, [[1, 128], [1, 256]])
                t_h = DMA(P, "sp", Htab[:, (mix * 8 + h) * 3 + kind, :], src, ds_h,
                          waits=[t_p2_pe, t_p2_last_evac, t_fv])
    ds_pn = P.dsem("ds_pastneg")

    if stage == 25:
        return _finish()

    st = {"rb": 0, "wb": 0, "bc_free": None, "rd_free": [None, None], "w_free": [None, None],
          "o_free": [None, None]}
    BCB = 6

    def stage_D(bankP, t_pv, sink_ap):
        rb = st["rb"] % 2
        st["rb"] += 1
        if sink_ap is not None:
            t1 = ACTF(P, rd[rb][64:65, 0:256], PS[bankP][64:65, 0:256], AF.Identity, bias=sink_ap, scale=1.0,
                      waits=[t_pv, st["rd_free"][rb], t_es])
            t2 = P.op("dve", lambda e, o=rd[rb][64:65, 0:256]: e.reciprocal(out=o, in_=o), waits=[t1])
        else:
            t2 = P.op("dve", lambda e, o=rd[rb][64:65, 0:256], i=PS[bankP][64:65, 0:256]: e.reciprocal(out=o, in_=i),
                      waits=[t_pv, st["rd_free"][rb]])
        return rb, t2

    def stage_E(rb, t_rd):
        t = MM(P, PS[BCB][:, 0:256], ones_f[64:65, 0:128], rd[rb][64:65, 0:256], True, True,
               waits=[t_rd, bank_free[BCB], t_m1], inc=True)
        st["rd_free"][rb] = t
        return t

    def stage_F(bankP, t_bc, Zt, half, swa=False, t_pv=None):
        wb = st["wb"] % 2
        st["wb"] += 1
        rows = slice(half * 64, half * 64 + 64)
        if swa:
            t4a = COPY(P, "act", otmp[wb][rows, 0:256], PS[bankP][0:64, 0:256], waits=[t_pv, st["o_free"][wb]])
            bank_free[bankP] = t4a
            t3 = TT(P, "dve", wtmp[wb][rows, 0:256], Zt[rows, :], PS[BCB][rows, 0:256], ALU.mult,
                    waits=[t_bc, st["w_free"][wb]])
            bank_free[BCB] = t3
            t4 = TT(P, "pool", Zt[rows, :], otmp[wb][rows, 0:256], wtmp[wb][rows, 0:256], ALU.mult, waits=[t3, t4a])
            st["o_free"][wb] = t4
            st["w_free"][wb] = t4
            return t4
        t3 = TT(P, "dve", wtmp[wb][rows, 0:256], Zt[rows, :], PS[BCB][rows, 0:256], ALU.mult,
                waits=[t_bc, st["w_free"][wb]])
        bank_free[BCB] = t3
        if half == 0:
            t4 = TT(P, "dve", Zt[rows, :], PS[bankP][0:64, 0:256], wtmp[wb][rows, 0:256], ALU.mult, waits=[t3])
        else:
            t4a = COPY(P, "dve", otmp[wb][rows, 0:256], PS[bankP][0:64, 0:256], waits=[st["o_free"][wb]])
            t4 = TT(P, "dve", Zt[rows, :], otmp[wb][rows, 0:256], wtmp[wb][rows, 0:256], ALU.mult, waits=[t3, t4a])
            st["o_free"][wb] = t4
        st["w_free"][wb] = t4
        bank_free[bankP] = t4
        return t4

    KA1 = XT[:, 0:8192]
    w_ka = [t_p2_pe, t_p2_last_evac]
    P.op("pool", lambda e: e.memset(KA1[0:64, :], 0.0), waits=w_ka, inc=False)
    t_c1 = COPY(P, "pool", KA1[64:128, :], KA[64:128, :], waits=w_ka)
    t_kap = P.op("pool", lambda e: e.memset(KA[64:128, :], 0.0), waits=[t_c1])
    P.op("pool", lambda e: e.memset(rd[0][:, :], 1.0), waits=w_ka + [t_qsp[NSLOT - 2], t_qsp[NSLOT - 1]], inc=False)
    t_rdinit = P.op("pool", lambda e: e.memset(rd[1][:, :], 1.0), waits=w_ka)
    KAp = [KA, KA1]
    swa_rh_free = [None, None]
    if stage == 28:
        return _finish()
    units = [(i, m) for i in range(NSLOT) for m in range(4)]
    if stage in (29, 291, 292, 293):
        units = units[:6]
    pt_free = [None, None, None]
    sw = {}
    scnt = 0
    pcnt = 0
    NU = len(units)
    t_bc = None
    for n in range(NU + 2):
        if n < NU:
            i, m = units[n]
            g = m // 2
            pa = (2 * m) % 4
            rows = slice(g * 64, g * 64 + 64)
            tok0 = (2 * i + 1) * 256
            qs = i * 256
            sbk = scnt % 2
            pb = scnt % 3
            scnt += 1
            w0 = [bank_free[2 * sbk], bank_free[2 * sbk + 1], t_h, t_cb, t_p2_last_evac, t_kap]
            t_s = None
            for e_ in range(2):
                h = 2 * m + e_
                off = e_ * 512
                MM(P, PSB[sbk][:, off:off + 256], KAp[g][:, tok0:tok0 + 128], QA[:, pa + e_, qs:qs + 256], True, False,
                   waits=w0 if e_ == 0 else [])
                MM(P, PSB[sbk][:, off:off + 256], antiid[:, :], Htab[:, h * 3 + 0, :], False, True)
                MM(P, PSB[sbk][:, off + 256:off + 384], KAp[g][:, tok0 + 128:tok0 + 256], QA[:, pa + e_, qs + 128:qs + 256], True, False)
                MM(P, PSB[sbk][:, off + 256:off + 384], antiid[:, :], Htab[:, h * 3 + 1, 128:256], False, True)
                MM(P, PSB[sbk][:, off + 384:off + 512], KAp[g][:, tok0 - 128:tok0], QA[:, pa + e_, qs:qs + 128], True, False)
                t_s = MM(P, PSB[sbk][:, off + 384:off + 512], antiid[:, :], Htab[:, h * 3 + 2, 0:128], False, True, inc=(e_ == 1))
            if i == 0:
                ACTF(P, PT[pb][:, 0:384], PSB[sbk][:, 0:384], AF.Exp, scale=0.125, waits=[t_s, pt_free[pb]], inc=False)
                ACTF(P, PT[pb][:, 512:896], PSB[sbk][:, 512:896], AF.Exp, scale=0.125, inc=False)
                ACTF(P, PT[pb][:, 384:512], PSB[sbk][:, 384:512], AF.Exp, bias=firstneg[:, 0:1], scale=0.125, waits=[t_c], inc=False)
                t_e = ACTF(P, PT[pb][:, 896:1024], PSB[sbk][:, 896:1024], AF.Exp, bias=firstneg[:, 0:1], scale=0.125)
            else:
                t_e = ACTF(P, PT[pb][:, :], PSB[sbk][:, :], AF.Exp, scale=0.125, waits=[t_s, pt_free[pb]])
            bank_free[2 * sbk] = t_e
            bank_free[2 * sbk + 1] = t_e
            sw[n] = dict(pb=pb, t_e=t_e, i=i, m=m, g=g, pa=pa)
        if 2 <= n and stage not in (291, 292):
            u2 = sw[n - 2]
            rb2 = u2["rb"]
            t_mv = COPY(P, "act", rd[rb2][64:65, 256:512], rd[rb2][0:1, 0:256], waits=[u2["t_rc"]])
            t_hi = COPY(P, "dve", rdh[rb2][64:65, 0:512], rd[rb2][64:65, 0:512], waits=[t_mv, swa_rh_free[rb2]])
            t_lo = TT(P, "pool", rdl[rb2][64:65, 0:512], rd[rb2][64:65, 0:512], rdh[rb2][64:65, 0:512], ALU.subtract,
                      waits=[t_hi])
            st["rd_free"][rb2] = t_lo
            u2["t_rd"] = t_lo
        if 1 <= n <= NU and stage != 291:
            u = sw[n - 1]
            i, m, g, pb = u["i"], u["m"], u["g"], u["pb"]
            T0 = (2 * i + 1) * 2
            bp = 4 + pcnt % 2
            pcnt += 1
            t_pv = None
            for e_ in range(2):
                c0 = e_ * 256
                o0 = e_ * 512
                MM(P, PS[bp][0:65, c0:c0 + 256], VA[:, T0, g, :], PT[pb][:, o0:o0 + 256], True, False,
                   waits=[u["t_e"], bank_free[bp]] if e_ == 0 else [])
                MM(P, PS[bp][0:65, c0 + 128:c0 + 256], VA[:, T0 + 1, g, :], PT[pb][:, o0 + 256:o0 + 384], False, False)
                t_pv = MM(P, PS[bp][0:65, c0:c0 + 128], VA[:, T0 - 1, g, :], PT[pb][:, o0 + 384:o0 + 512], False, True,
                          inc=(e_ == 1))
            pt_free[pb] = t_pv
            rb = st["rb"] % 2
            st["rb"] += 1
            ACTF(P, rd[rb][64:65, 0:256], PS[bp][64:65, 0:256], AF.Identity, bias=es_sb[64:65, 2 * m:2 * m + 1], scale=1.0,
                 waits=[t_pv, st["rd_free"][rb], t_es, t_rdinit], inc=False)
            t_d1 = ACTF(P, rd[rb][0:1, 0:256], PS[bp][64:65, 256:512], AF.Identity,
                        bias=es_sb[64:65, 2 * m + 1:2 * m + 2], scale=1.0)
            t_rc = P.op("dve", lambda e, o=rd[rb][0:65, 0:256]: e.reciprocal(out=o, in_=o), waits=[t_d1])
            u.update(bp=bp, rb=rb, t_rc=t_rc, t_pv=t_pv)
        if 2 <= n and stage not in (291, 292):
            u = sw[n - 2]
            rb, bp, g, pa, i = u["rb"], u["bp"], u["g"], u["pa"], u["i"]
            rows = slice(g * 64, g * 64 + 64)
            MM(P, PS[BCB][:, 0:512], halfs_bf[64:65, 0:128], rdh[rb][64:65, 0:512], True, False,
               waits=[u["t_rd"], bank_free[BCB], t_hb])
            t_bc = MM(P, PS[BCB][:, 0:512], halfs_bf[64:65, 0:128], rdl[rb][64:65, 0:512], False, True, inc=True)
            swa_rh_free[rb] = t_bc
            if stage == 293:
                bank_free[BCB] = None
                continue
            wb = st["wb"] % 2
            st["wb"] += 1
            Zp = ZAv[rows, pa:pa + 2, i * 256:(i + 1) * 256]
            t4a = COPY(P, "act", otmp[wb][rows, 0:512], PS[bp][0:64, 0:512], waits=[u["t_pv"], st["o_free"][wb]])
            bank_free[bp] = t4a
            t3 = TT(P, "dve", wtmp[wb][rows, 0:512].rearrange("p (a t) -> p a t", a=2), Zp,
                    PS[BCB][rows, 0:512].rearrange("p (a t) -> p a t", a=2), ALU.mult,
                    waits=[t_bc, st["w_free"][wb]])
            bank_free[BCB] = t3
            t4 = TT(P, "pool", Zp, otmp[wb][rows, 0:512].rearrange("p (a t) -> p a t", a=2),
                    wtmp[wb][rows, 0:512].rearrange("p (a t) -> p a t", a=2), ALU.mult, waits=[t3, t4a])
            st["o_free"][wb] = t4
            st["w_free"][wb] = t4
    t_swa_done_pe = t_bc
    t_swa_pool = (P.psem["pool"], P.pcount["pool"])
    if stage in (3, 29, 291, 292, 293):
        return _finish()

    DMA(P, "sp", XT[:, 0:1024].bitcast(F32), pastneg_d.ap(), ds_pn, waits=[t_swa_done_pe])
    t_pn = DMA(P, "sp", XT[:, 1024:2048].bitcast(F32), ownsel_d.ap(), ds_pn, waits=[t_swa_done_pe])
    ds_kb = [P.dsem("ds_kb0"), P.dsem("ds_kb1")]
    ds_vb = [P.dsem("ds_vb0"), P.dsem("ds_vb1")]
    ds_qb = [P.dsem("ds_qb0"), P.dsem("ds_qb1")]
    ds_bi = P.dsem("ds_blkind")
    t_bi = None
    for b in range(2):
        t_bi = DMA(P, "pool", KB[b][64:96, :], blkind_d.ap(), ds_bi, waits=[t_swa_done_pe])
    kb_h = kb_s.ap().rearrange("(h d) t -> h d t", d=64)
    qb_h = qb_s.ap().rearrange("(h d) t -> h d t", d=64)
    head_pe_done = {}
    head_loads = {}
    all_spills = [t_ksp[14], t_ksp[15], t_vsp[14], t_vsp[15], t_qsp[14], t_qsp[15]]

    def moba_loads(h):
        bb = h % 2
        w = [head_pe_done.get(h - 2), t_swa_done_pe] + all_spills
        t3 = DMA(P, "sp", QB[bb][0:64, :], qb_h[h], ds_qb[bb], waits=w)
        t1 = DMA(P, "sp", KB[bb][0:64, :], kb_h[h], ds_kb[bb], waits=w)
        t2 = DMA(P, "sp", VB[bb], vb_s.ap()[h], ds_vb[bb], waits=w)
        head_loads[h] = [t1, t2, t3]

    prep_done = {}

    def moba_prep(h):
        bb = h % 2
        lw = head_loads[h]
        kmh = kmean[0:64, h * 32:(h + 1) * 32]
        bk = 7
        tg = []
        for half in range(2):
            for t in range(16):
                tt = half * 16 + t
                tm = MM(P, PS[bk][:, t * 32:(t + 1) * 32], QB[bb][0:64, tt * 128:(tt + 1) * 128], kmh, True, True,
                        waits=[lw[2], t_kmean, bank_free[bk]] if t == 0 else [], inc=(t == 15))
            g = TT(P, "dve", gm[:, half * 16:(half + 1) * 16, :].rearrange("p (s two) v -> p s two v", two=2),
                   PS[bk][:, :].rearrange("p (s two v) -> p s two v", two=2, v=32),
                   pastneg[:, half * 8:(half + 1) * 8, :].unsqueeze(2).broadcast_to([128, 8, 2, 32]),
                   ALU.add, waits=[tm, t_pn, prep_done.get(h - 1)])
            bank_free[bk] = g
            tg.append(g)
        tmx = None
        for tt in range(32):
            tmx = P.op("dve", lambda e, o=top8[:, tt, :], i_=gm[:, tt, :]: e.max(out=o, in_=i_),
                       waits=tg if tt == 0 else [], inc=(tt == 31))
        t1 = TS(P, "dve", thr[:, 0:32], top8[:, :, 2], -1e8, None, ALU.max, waits=[tmx])
        t2 = TT(P, "dve", sel[:, :, :], gm[:, :, :], thr[:, 0:32].unsqueeze(2).broadcast_to([128, 32, 32]),
                ALU.is_ge, waits=[t1])
        t3 = TT(P, "dve", sel[:, :, :].rearrange("p (s two) v -> p s two v", two=2),
                sel[:, :, :].rearrange("p (s two) v -> p s two v", two=2),
                ownsel[:, :, :].unsqueeze(2).broadcast_to([128, 16, 2, 32]), ALU.max, waits=[t2])
        t4 = TS(P, "dve", nm[:, :, :], sel[:, :, :], -NEGM, NEGM, ALU.mult, ALU.add, waits=[t3])
        prep_t4[h] = t4

    prep_t4 = {}

    def moba_prep_b_step(h, g8):
        bb = h % 2
        bk = 7
        for q4 in range(4):
            tt = g8 * 4 + q4
            tm = MM(P, PS[bk][0:32, q4 * 128:(q4 + 1) * 128], nm[:, tt, :], ident[:, :], True, True,
                    waits=[prep_t4[h], t_cb, bank_free[bk]] if q4 == 0 else [], inc=(q4 == 3))
        tc = COPY(P, "dve", QB[bb][64:96, g8 * 512:(g8 + 1) * 512], PS[bk][0:32, :], waits=[tm])
        bank_free[bk] = tc
        if g8 == 7:
            prep_done[h] = tc

    def moba_prep_b(h):
        for g8 in range(8):
            moba_prep_b_step(h, g8)

    mst = {"u": 0, "pv": 0}

    def moba_attention(h, mid_hook):
        bb = h % 2
        half, p = h % 2, h // 2
        lw = head_loads[h] + [prep_done[h], t_bi, t_h, t_cb]
        ulist = [(i, w) for i in range(NSLOT) for w in range(i + 1)]
        pend = {}
        tails = []
        NUu = len(ulist)
        LAG = 2
        for n in range(NUu + LAG):
            if n < NUu:
                i, w = ulist[n]
                sb = mst["u"] % 2
                pb = mst["u"] % 3
                mst["u"] += 1
                qs = i * 256
                lastw = (w == i)
                first = True
                for vv in range(2):
                    v = 2 * w + vv
                    for a in range(2):
                        kt = v * 256 + a * 128
                        col = (vv * 2 + a) * 256
                        bkk = 2 * sb + vv
                        special = lastw and (vv == 1 or a == 1)
                        fin = (vv == 1 and a == 1)
                        MM(P, PSB[sb][:, col:col + 256], KB[bb][0:96, kt:kt + 128], QB[bb][0:96, qs:qs + 256],
                           True, not special, waits=(lw + [bank_free[2 * sb], bank_free[2 * sb + 1]]) if first else [],
                           inc=(fin and not special))
                        first = False
                        if special:
                            kind = a if vv == 1 else 2
                            MM(P, PSB[sb][:, col:col + 256], antiid[:, :], Htab[:, (8 + h) * 3 + kind, :],
                               False, True, inc=fin)
                t_s = (P.psem["pe"], P.pcount["pe"])
                t_e = ACTF(P, PT[pb][:, :], PSB[sb][:, :], AF.Exp, scale=0.125, waits=[t_s, pt_free[pb]])
                bank_free[2 * sb] = t_e
                bank_free[2 * sb + 1] = t_e
                pend[n] = dict(pb=pb, t_e=t_e)
            if n >= LAG:
                i, w = ulist[n - LAG]
                u = pend.pop(n - LAG)
                if w == 0:
                    mst["bp"] = 4 + mst["pv"] % 2
                    mst["pv"] += 1
                    while any(tl[2] == mst["bp"] for tl in tails):
                        tails.pop(0)[1]()
                bp = mst["bp"]
                lastw = (w == i)
                for vv in range(2):
                    v = 2 * w + vv
                    for a in range(2):
                        col = (vv * 2 + a) * 256
                        fin = (vv == 1 and a == 1)
                        t_pv = MM(P, PS[bp][0:65, 0:256], VB[bb][:, v * 2 + a, :], PT[u["pb"]][:, col:col + 256],
                                  (w == 0 and vv == 0 and a == 0), (lastw and fin),
                                  waits=[u["t_e"], bank_free[bp]] if (vv == 0 and a == 0) else [], inc=fin)
                pt_free[u["pb"]] = t_pv
                if lastw:
                    rb, t_rd = stage_D(bp, t_pv, None)
                    Zt = ZBv[:, p, i * 256:(i + 1) * 256]
                    tails.append([5, (lambda rb=rb, t_rd=t_rd, bp=bp, Zt=Zt: stage_F(bp, stage_E(rb, t_rd), Zt, half)), bp])
                    if mid_hook is not None:
                        mid_hook(i)
            for tl in tails:
                tl[0] -= 1
            while tails and tails[0][0] <= 0:
                tails.pop(0)[1]()
        for tl in tails:
            tl[1]()
        head_pe_done[h] = (P.psem["pe"], P.pcount["pe"])

    moba_loads(0)
    if stage == 31:
        return _finish()
    moba_prep(0)
    moba_prep_b(0)
    if stage == 32:
        return _finish()
    moba_loads(1)
    for h in range(8):
        def hook(i, h=h):
            if h + 1 < 8:
                if i == 4:
                    moba_prep(h + 1)
                if 7 <= i < 15:
                    moba_prep_b_step(h + 1, i - 7)
        use_hook = stage not in (33, 36)
        if not use_hook and h + 1 < 8 and stage == 36:
            moba_prep(h + 1)
            moba_prep_b(h + 1)
        moba_attention(h, hook if use_hook else None)
        if stage in (33, 34):
            return _finish()
        if stage == 36 and h == 1:
            return _finish()
        if stage >= 40 and stage < 48 and h == stage - 40:
            return _finish()
        if h + 2 < 8:
            moba_loads(h + 2)
    t_moba_pe = head_pe_done[7]
    t_moba_dve = (P.psem["dve"], P.pcount["dve"])

    if stage == 35:
        return _finish()

    p3_done = [t_moba_pe, t_moba_dve, t_swa_pool]
    ds_w4 = P.dsem("ds_w4")
    ds_w4_dummy = None
    ds_gb = P.dsem("ds_gb")
    DMA(P, "sp", gam, bass.AP(lng_d, 0, [[0, 128], [1, D]]), ds_gb, waits=p3_done)
    t_gb = DMA(P, "sp", bet, bass.AP(lnb_d, 0, [[0, 128], [1, D]]), ds_gb, waits=p3_done)
    ds_xr = [P.dsem("ds_xr0"), P.dsem("ds_xr1")]
    ds_out = [P.dsem(f"ds_out{k}") for k in range(4)]
    pe_g_done = {}
    t_out_done = None
    t_tmp_free = [None] * 4
    xr_free = [None, None]
    yt_free = [None] * 4
    out_toks = []
    lncnt = 0
    lnm = lnst[:, :].rearrange("p (t k) -> p t k", k=6)
    t_x4 = {}

    def load_x4(tc):
        b = tc % 2
        w = [pe_g_done.get(tc - 2)] + p3_done
        DMA(P, "pool", xt[b][:, :, 0:256], xT_v[:, :, (4 * tc + 1) * 256:(4 * tc + 2) * 256], ds_xt[b], waits=w)
        t_x4[tc] = DMA(P, "pool", xt[b][:, :, 256:512], xT_v[:, :, (4 * tc + 3) * 256:(4 * tc + 4) * 256], ds_xt[b], waits=w)

    load_x4(0)
    ds_wj = [P.dsem(f"ds_wj{j}") for j in range(8)]
    t_wj = []
    for j in range(8):
        DMA(P, "pool", Wg[:, :, j * 128:(j + 1) * 128], wg_d.ap()[j], ds_wj[j], waits=p3_done)
        DMA(P, "pool", Wg[:, :, 1024 + j * 128:1024 + (j + 1) * 128], wg_d.ap()[8 + j], ds_wj[j], waits=p3_done)
        DMA(P, "pool", Woa[:, :, j * 128:(j + 1) * 128], woa_d.ap()[j], ds_wj[j], waits=p3_done)
        t_wj.append(DMA(P, "pool", Wob[:, :, j * 128:(j + 1) * 128], wob_d.ap()[j], ds_wj[j], waits=p3_done))
    t_w4 = DMA(P, "pool", Wo, wo_d.ap().rearrange("(k p) n -> p k n", p=128), ds_w4, waits=p3_done)
    for tc in range(8):
        b = tc % 2
        if tc + 1 < 8:
            load_x4(tc + 1)
        t_x = t_x4[tc]
        tsl = slice(tc * 512, (tc + 1) * 512)
        for j in range(8):
            bA, bB = j % 2, 2 + j % 2
            for kc in range(8):
                t_gA = MM(P, PS[bA][:, :], Wg[:, kc, j * 128:(j + 1) * 128], xt[b][:, kc, :], kc == 0, kc == 7,
                          waits=[t_x, t_wj[j], bank_free[bA]] if kc == 0 else [], inc=(kc == 7))
            for kc in range(8):
                t_gB = MM(P, PS[bB][:, :], Wg[:, kc, 1024 + j * 128:1024 + (j + 1) * 128], xt[b][:, kc, :], kc == 0, kc == 7,
                          waits=[bank_free[bB]] if kc == 0 else [], inc=(kc == 7))
            for k in range(4):
                t_uA = MM(P, PS[4][:, :], Woa[:, k, j * 128:(j + 1) * 128], ZAv[:, k, tsl], k == 0, k == 3,
                          waits=[bank_free[4]] if k == 0 else [], inc=(k == 3))
            for k in range(4):
                t_uB = MM(P, PS[5][:, :], Wob[:, k, j * 128:(j + 1) * 128], ZBv[:, k, tsl], k == 0, k == 3,
                          waits=[bank_free[5]] if k == 0 else [], inc=(k == 3))
            sig = []
            for (bk, tg, ti, col) in ((bA, t_gA, 0, j), (bB, t_gB, 1, 8 + j)):
                t1 = ACTF(P, tmp4[ti], PS[bk][:, :], AF.Tanh, bias=hbg[:, col:col + 1], scale=0.5,
                          waits=[tg, t_tmp_free[ti], t_nb] + p3_done)
                bank_free[bk] = t1
                t3 = ACTF(P, tmp4[ti], tmp4[ti], AF.Identity, bias=half_c[:, 0:1], scale=0.5, waits=[t1])
                sig.append(t3)
            tA = TT(P, "dve", tmp4[2], tmp4[0], PS[4][:, :], ALU.mult, waits=[sig[0], t_uA, t_tmp_free[2]] + p3_done)
            bank_free[4] = tA
            t_tmp_free[0] = tA
            tB = TT(P, "dve", tmp4[3], tmp4[1], PS[5][:, :], ALU.mult, waits=[sig[1], t_uB, t_tmp_free[3]])
            bank_free[5] = tB
            t_tmp_free[1] = tB
            tM = TT(P, "dve", MG[:, j, :], tmp4[2], tmp4[3], ALU.add, waits=[tA, tB, t_out_done])
            t_tmp_free[2] = tM
            t_tmp_free[3] = tM
        pe_g_done[tc] = t_uB
        t_ag = None
        for tt in range(4):
            r0 = tc * 512 + tt * 128
            xb = lncnt % 2
            lncnt += 1
            t_xr = DMA(P, "sp", xr[xb], xown_d.ap()[r0:r0 + 128, :], ds_xr[xb], waits=[xr_free[xb]] + p3_done)
            t_o = []
            for hf in range(2):
                bk = 6 + hf
                for jj in range(8):
                    t_mm = MM(P, PS[bk][:, :], MG[:, jj, tt * 128:(tt + 1) * 128], Wo[:, jj, hf * 512:(hf + 1) * 512],
                              jj == 0, jj == 7, waits=[tM, t_w4, bank_free[bk]] if jj == 0 else [], inc=(jj == 7))
                t_o.append(t_mm)
            t_out_done = t_o[1]
            ty = None
            for hf in range(2):
                cs = slice(hf * 512, (hf + 1) * 512)
                ty = STT(P, YT[tt][:, cs], xr[xb][:, cs], ALPHA, PS[6 + hf][:, :], ALU.mult, ALU.add,
                         waits=[t_o[hf], t_xr, yt_free[tt]])
                bank_free[6 + hf] = ty
            xr_free[xb] = ty
            P.op("dve", lambda e, o=lnst[:, 0:6], i_=YT[tt][:, 0:512]: e.bn_stats(out=o, in_=i_), waits=[ty, t_ag], inc=False)
            t_s2 = P.op("dve", lambda e, o=lnst[:, 6:12], i_=YT[tt][:, 512:1024]: e.bn_stats(out=o, in_=i_))
            t_ag = P.op("dve", lambda e, o=lnmv[:, 2 * tt:2 * tt + 2], i_=lnst[:, 0:12]: e.bn_aggr(out=o, in_=i_), waits=[t_s2])
        mv = lnmv[:, 0:8].rearrange("p (t k) -> p t k", k=2)
        t_ve = TS(P, "dve", lnmv[:, 8:12], mv[:, :, 1], LN_EPS, None, ALU.add, waits=[t_ag, yt_free[3]])
        t_l1 = ACTF(P, lnmv[:, 8:12], lnmv[:, 8:12], AF.Ln, waits=[t_ve])
        t_l2 = ACTF(P, lnmv[:, 8:12], lnmv[:, 8:12], AF.Exp, scale=-0.5, waits=[t_l1])
        for tt in range(4):
            r0 = tc * 512 + tt * 128
            t_n1 = STT(P, YT[tt][:, :], YT[tt][:, :], lnmv[:, 2 * tt:2 * tt + 1], gam, ALU.subtract, ALU.mult,
                       waits=[t_l2, t_gb])
            t_n2 = STT(P, YT[tt][:, :], YT[tt][:, :], lnmv[:, 8 + tt:9 + tt], bet, ALU.mult, ALU.add, waits=[t_n1])
            t_st = DMA(P, "sp", out_d.ap()[r0:r0 + 128, :], YT[tt], ds_out[tt], waits=[t_n2])
            yt_free[tt] = t_st
            out_toks.append(t_st)
    P.wait("sp", out_toks[-4:])
    P.emit()
    P.close()
    return nc


def _t5_bucket_np(d):
    n = np.maximum(d, 0)
    nf = np.maximum(n, 1).astype(np.float32)
    large = 16 + (np.log(nf / np.float32(16)) / np.float32(math.log(128 / 16)) * np.float32(16)).astype(np.int32)
    large = np.minimum(large, 31)
    return np.where(n < 16, n, large)


def _constants(par):
    i = np.arange(FL)
    d = i - 255
    bkt = _t5_bucket_np(d)
    ohA = np.zeros((32, FL), np.float32)
    ohB = np.zeros((32, FL), np.float32)
    vA = (d >= 0) & (d <= 127)
    vB = d >= 0
    ohA[bkt[vA], i[vA]] = 1.0
    ohB[bkt[vB], i[vB]] += 1.0
    ohB[31, i[vB]] -= 1.0
    mkA = np.where(vA, 0.0, 8.0 * NEGM).astype(np.float32)[None].repeat(8, 0)
    mkB = np.where(vB, 0.0, 8.0 * NEGM).astype(np.float32)[None].repeat(8, 0)
    pastneg = np.full((NSLOT, 32), -1e9, np.float32)
    ownsel = np.zeros((NSLOT, 32), np.float32)
    for s in range(NSLOT):
        lo = 0 if par == 1 else 1
        pastneg[s, lo:2 * s + 1] = 0.0
        ownsel[s, 2 * s + 1] = 1.0
    pastneg = np.ascontiguousarray(np.broadcast_to(pastneg.reshape(1, -1), (128, NSLOT * 32)))
    ownsel = np.ascontiguousarray(np.broadcast_to(ownsel.reshape(1, -1), (128, NSLOT * 32)))
    firstneg = np.full((128, 1), NEGM if par == 0 else 0.0, np.float32)
    ident = np.eye(128, dtype=np.float32)
    antiid = np.ascontiguousarray(ident[::-1])
    blkind = np.zeros((32, SEQ), np.float32)
    for v in range(32):
        blkind[v, v * 256:(v + 1) * 256] = 1.0
    return dict(ohA=ohA, ohB=ohB, mkA=mkA, mkB=mkB, pastneg=pastneg, ownsel=ownsel, firstneg=firstneg,
                ident=ident, antiid=antiid, blkind=blkind)


def _weights(w_in, b_gate, sinks, rel_table, w_out_a, w_out_b, w_out, ln_gamma, ln_beta):
    w = np.asarray(w_in)[0]
    qA, kA, vA, zA = w[:, 0:512], w[:, 512:640], w[:, 640:768], w[:, 768:1280]
    qB, kB, vB, zB = w[:, 1280:1792], w[:, 1792:2304], w[:, 2304:2816], w[:, 2816:3328]
    gA, gB = w[:, 3328:4352], w[:, 4352:5376]

    def pair48(m):
        hm = m.reshape(m.shape[0], 8, 64)
        return np.concatenate([np.concatenate([hm[:, p], hm[:, p + 4]], axis=1) for p in range(4)], axis=1)

    w_kv = np.ascontiguousarray(np.concatenate([kA, kB, vA, vB], axis=1))
    w_own = np.ascontiguousarray(np.concatenate([pair48(qA), pair48(zA), qB, zB], axis=1))
    w_g = np.ascontiguousarray(np.concatenate([gA, gB], axis=1))
    woa = np.asarray(w_out_a)[0].reshape(8, 64, D)
    w_oa = np.ascontiguousarray(np.concatenate([np.concatenate([woa[p], woa[p + 4]], axis=0) for p in range(4)], axis=0))
    w_ob = np.ascontiguousarray(np.asarray(w_out_b)[0])
    w_o = np.ascontiguousarray(np.asarray(w_out)[0])
    bg = np.ascontiguousarray(np.asarray(b_gate)[0].reshape(16, 128).T)
    w_g = np.ascontiguousarray(w_g.reshape(8, 128, 16, 128).transpose(2, 1, 0, 3))
    w_oa = np.ascontiguousarray(w_oa.reshape(4, 128, 8, 128).transpose(2, 1, 0, 3))
    w_ob = np.ascontiguousarray(w_ob.reshape(4, 128, 8, 128).transpose(2, 1, 0, 3))
    return dict(w_kv=w_kv, w_own=w_own, w_g=w_g, w_oa=w_oa, w_ob=w_ob, w_o=w_o, bg=bg,
                rel=np.ascontiguousarray(np.asarray(rel_table), dtype=np.float32),
                sinks=np.ascontiguousarray(np.asarray(sinks)[0].reshape(1, 8)),
                lng=np.ascontiguousarray(np.asarray(ln_gamma)[0].reshape(1, D)),
                lnb=np.ascontiguousarray(np.asarray(ln_beta)[0].reshape(1, D)))


_NC_CACHE = {}


def kernel(x, w_in, b_gate, sinks, rel_table, w_out_a, w_out_b, w_out, ln_gamma, ln_beta):
    x = np.asarray(x, dtype=np.float32)
    wd = _weights(w_in, b_gate, sinks, rel_table, w_out_a, w_out_b, w_out, ln_gamma, ln_beta)
    wd = {k: np.asarray(v, dtype=np.float32) for k, v in wd.items()}
    in_maps = []
    for c in range(NCORES):
        b, par = c // 2, c % 2
        xb = x[b]
        if par == 1:
            virt = xb
        else:
            virt = np.concatenate([np.zeros((256, D), np.float32), xb[:SEQ - 256]], axis=0)
        xT = np.ascontiguousarray(virt.T)
        xown = np.ascontiguousarray(virt.reshape(32, 256, D)[1::2].reshape(NSLOT * 256, D))
        m = dict(xT=xT, xown=xown)
        m.update(wd)
        m.update(_constants(par))
        in_maps.append(m)
    if "nc" not in _NC_CACHE:
        _NC_CACHE["nc"] = build_nc()
    nc = _NC_CACHE["nc"]
    res = run_bass_kernel_spmd(nc, in_maps, core_ids=list(range(NCORES)))
    out = np.empty((4, SEQ, D), np.float32)
    for c in range(NCORES):
        b, par = c // 2, c % 2
        o = np.asarray(res.results[c]["out"]).reshape(NSLOT, 256, D)
        out[b].reshape(32, 256, D)[par::2] = o
    return out
```

```python
import math
from contextlib import ExitStack

import numpy as np
import concourse.bass as bass
import concourse.mybir as mybir
from concourse.bass_utils import run_bass_kernel_spmd

F32 = mybir.dt.float32
BF16 = mybir.dt.bfloat16
AF = mybir.ActivationFunctionType
ALU = mybir.AluOpType
AX = mybir.AxisListType

D = 1024
SEQ = 8192
NSLOT = 16
NCORES = 8
ALPHA = 2.0 ** 0.25
LN_EPS = 1e-5
NEGM = -30000.0
FL = 768


class DSem:
    def __init__(self, prog, name):
        self.h = prog.stack.enter_context(prog.nc.semaphore(name))
        self.count = 0


class _Dummy:
    def then_inc(self, *a):
        pass


class Prog:
    ENG = ("pe", "act", "dve", "pool", "sp")

    def __init__(self, nc):
        self.nc = nc
        self.stack = ExitStack()
        self.ops = {e: [] for e in self.ENG}
        self.psem = {e: self.stack.enter_context(nc.semaphore("prog_" + e)) for e in self.ENG}
        self.pcount = {e: 0 for e in self.ENG}
        self.nd = 0
        self.all_dsems = []

    def dsem(self, name=None):
        self.nd += 1
        d = DSem(self, name or f"dsem{self.nd}")
        self.all_dsems.append(d)
        return d

    def sb(self, name, shape, dt):
        return self.stack.enter_context(self.nc.sbuf_tensor(name, shape, dt))

    def ps(self, name, shape, dt):
        return self.stack.enter_context(self.nc.psum_tensor(name, shape, dt))

    def op(self, eng, fn, waits=(), inc=True):
        tok = None
        if inc:
            self.pcount[eng] += 1
            tok = (self.psem[eng], self.pcount[eng])
        self.ops[eng].append((fn, [w for w in waits if w is not None], (self.psem[eng], 1) if inc else None))
        return tok

    def dma(self, eng, fn, dsem, waits=()):
        dsem.count += 16
        tok = (dsem.h, dsem.count)
        self.ops[eng].append((fn, [w for w in waits if w is not None], (dsem.h, 16)))
        return tok

    def wait(self, eng, waits):
        self.ops[eng].append((None, [w for w in waits if w is not None], None))

    def emit(self):
        nc = self.nc
        with nc.Block() as block:
            def replay(engname):
                def run(e):
                    waited = {}
                    for fn, waits, inc in self.ops[engname]:
                        for (s, v) in waits:
                            k = id(s)
                            if waited.get(k, 0) >= v:
                                continue
                            waited[k] = v
                            e.wait_ge(s, v)
                        if fn is None:
                            continue
                        ins = fn(e)
                        if inc is not None:
                            ins.then_inc(inc[0], inc[1])
                return run
            block.tensor(replay("pe"))
            block.scalar(replay("act"))
            block.vector(replay("dve"))
            block.gpsimd(replay("pool"))
            block.sync(replay("sp"))

    def close(self):
        self.stack.close()


def MM(P, out, lhsT, rhs, start, stop, waits=(), inc=False):
    return P.op("pe", lambda e: e.matmul(out, lhsT=lhsT, rhs=rhs, start=start, stop=stop,
                                         skip_group_check=True), waits=waits, inc=inc)


def ACTF(P, out, in_, func, waits=(), bias=None, scale=1.0, inc=True):
    if bias is None:
        return P.op("act", lambda e: e.activation(out=out, in_=in_, func=func, scale=scale), waits=waits, inc=inc)
    return P.op("act", lambda e: e.activation(out=out, in_=in_, func=func, bias=bias, scale=scale),
                waits=waits, inc=inc)


def COPY(P, eng, out, in_, waits=(), inc=True):
    if eng == "act":
        return P.op("act", lambda e: e.copy(out=out, in_=in_), waits=waits, inc=inc)
    return P.op(eng, lambda e: e.tensor_copy(out=out, in_=in_), waits=waits, inc=inc)


def TT(P, eng, out, in0, in1, op, waits=(), inc=True):
    return P.op(eng, lambda e: e.tensor_tensor(out=out, in0=in0, in1=in1, op=op), waits=waits, inc=inc)


def TS(P, eng, out, in0, s1, s2, op0, op1=None, waits=(), inc=True):
    if op1 is None:
        return P.op(eng, lambda e: e.tensor_scalar(out=out, in0=in0, scalar1=s1, scalar2=None, op0=op0),
                    waits=waits, inc=inc)
    return P.op(eng, lambda e: e.tensor_scalar(out=out, in0=in0, scalar1=s1, scalar2=s2, op0=op0, op1=op1),
                waits=waits, inc=inc)


def STT(P, out, in0, scalar, in1, op0, op1, waits=(), inc=True):
    return P.op("dve", lambda e: e.scalar_tensor_tensor(out=out, in0=in0, scalar=scalar, in1=in1, op0=op0, op1=op1),
                waits=waits, inc=inc)


def DMA(P, eng, out, in_, dsem, waits=()):
    return P.dma(eng, lambda e: e.dma_start(out=out, in_=in_), dsem, waits=waits)


def build_nc(stage=99):
    nc = bass.Bass("TRN2", target_bir_lowering=False)
    dt_in = lambda n, s, d=F32: nc.dram_tensor(n, s, d, kind="ExternalInput")
    xT_d = dt_in("xT", [D, SEQ])
    xown_d = dt_in("xown", [NSLOT * 256, D])
    wkv_d = dt_in("w_kv", [D, 1280])
    wown_d = dt_in("w_own", [D, 2048])
    wg_d = dt_in("w_g", [16, 128, 8, 128])
    woa_d = dt_in("w_oa", [8, 128, 4, 128])
    wob_d = dt_in("w_ob", [8, 128, 4, 128])
    wo_d = dt_in("w_o", [D, D])
    bg_d = dt_in("bg", [128, 16])
    rel_d = dt_in("rel", [32, 16])
    sinks_d = dt_in("sinks", [1, 8])
    lng_d = dt_in("lng", [1, D])
    lnb_d = dt_in("lnb", [1, D])
    ohA_d = dt_in("ohA", [32, FL])
    ohB_d = dt_in("ohB", [32, FL])
    mkA_d = dt_in("mkA", [8, FL])
    mkB_d = dt_in("mkB", [8, FL])
    pastneg_d = dt_in("pastneg", [128, 512])
    ownsel_d = dt_in("ownsel", [128, 512])
    firstneg_d = dt_in("firstneg", [128, 1])
    ident_d = dt_in("ident", [128, 128])
    antiid_d = dt_in("antiid", [128, 128])
    blkind_d = dt_in("blkind", [32, SEQ])
    out_d = nc.dram_tensor("out", [NSLOT * 256, D], F32, kind="ExternalOutput")
    kb_s = nc.dram_tensor("kb_s", [512, SEQ], BF16, kind="Internal")
    vb_s = nc.dram_tensor("vb_s", [8, 128, 64, 65], BF16, kind="Internal")
    qb_s = nc.dram_tensor("qb_s", [512, NSLOT * 256], BF16, kind="Internal")
    fv_s = nc.dram_tensor("fv_s", [16, FL], BF16, kind="Internal")

    P = Prog(nc)
    ZA = P.sb("ZA", [128, 16384], BF16)
    ZB = P.sb("ZB", [128, 16384], BF16)
    R1 = P.sb("R1", [128, 33792], BF16)
    R2 = P.sb("R2", [128, 20480], BF16)
    XT = P.sb("XT", [128, 8192], BF16)
    QST = P.sb("QST", [128, 2048], BF16)
    ident = P.sb("ident_bf", [128, 128], BF16)
    antiid = P.sb("antiid_bf", [128, 128], BF16)
    ones_f = P.sb("ones_f", [128, 128], F32)
    rel_sb = P.sb("rel_sb", [32, 16], F32)
    bg_sb = P.sb("bg_sb", [128, 16], F32)
    negbg = P.sb("negbg", [128, 16], F32)
    sink_sb = P.sb("sink_sb", [128, 8], F32)
    es_sb = P.sb("es_sb", [128, 8], F32)
    firstneg = P.sb("firstneg_sb", [128, 1], F32)
    ksum = P.sb("ksum", [128, 4 * 32], F32)
    kmean = P.sb("kmean", [64, 8 * 32], BF16)
    lnst = P.sb("lnst", [128, 2 * 12], F32)
    lnmv = P.sb("lnmv", [128, 12], F32)
    YTB = P.sb("YTB", [128, 2048], F32)
    PSB = [P.ps(f"psb{i}", [128, 1024], F32) for i in range(4)]
    PS = [PSB[i // 2][:, (i % 2) * 512:(i % 2 + 1) * 512] for i in range(8)]
    half_c = P.sb("half_c", [128, 1], F32)
    halfs_bf = P.sb("halfs_bf", [128, 128], BF16)
    hbg = P.sb("hbg", [128, 16], F32)

    def v3(ap, **kw):
        pat = kw.pop("pat")
        return ap.rearrange(pat, **kw)

    Wkv = v3(ZA[:, 0:10240], pat="p (k n) -> p k n", k=8)
    kst = [v3(ZB[:, b * 2048:(b + 1) * 2048], pat="p (a t) -> p a t", a=4) for b in range(2)]
    vst = [v3(ZB[:, 4096 + b * 2080:4096 + (b + 1) * 2080], pat="p (h t c) -> p h t c", h=8, t=4) for b in range(2)]
    KA = R1[:, 0:8192]
    VA = v3(R1[:, 8192:8192 + 8320], pat="p (t g c) -> p t g c", g=2, c=65)
    QA = v3(R1[:, 16512:16512 + 16384], pat="p (a t) -> p a t", a=4)
    Wown = v3(R2[:, 0:16384], pat="p (k n) -> p k n", k=8)
    etmp = [R2[:, 16384 + b * 1024:16384 + (b + 1) * 1024].bitcast(F32) for b in range(2)]
    xt = [v3(XT[:, b * 4096:(b + 1) * 4096], pat="p (k t) -> p k t", k=8) for b in range(2)]
    xq = [v3(XT[:, b * 4096:b * 4096 + 2048], pat="p (k t) -> p k t", k=8) for b in range(2)]
    qst = [v3(QST[:, b * 1024:(b + 1) * 1024], pat="p (a t) -> p a t", a=4) for b in range(2)]
    ZAv = v3(ZA[:, :], pat="p (a t) -> p a t", a=4)
    ZBv = v3(ZB[:, :], pat="p (a t) -> p a t", a=4)
    S0 = 16512
    ohA_sb = R1[0:32, S0:S0 + 2 * FL].bitcast(F32)
    ohB_sb = R1[0:32, S0 + 2 * FL:S0 + 4 * FL].bitcast(F32)
    mkA_sb = R1[0:8, S0 + 4 * FL:S0 + 6 * FL].bitcast(F32)
    mkB_sb = R1[0:8, S0 + 6 * FL:S0 + 8 * FL].bitcast(F32)
    fsbA = R1[0:8, S0 + 8 * FL:S0 + 9 * FL]
    fsbB = R1[0:8, S0 + 9 * FL:S0 + 10 * FL]
    Htab = v3(R2[:, 0:12288], pat="p (n q) -> p n q", q=256)
    PT = [R2[:, 12288 + b * 1024:12288 + (b + 1) * 1024] for b in range(3)]
    wtmp = [R2[:, 15360 + b * 1024:15360 + (b + 1) * 1024].bitcast(F32) for b in range(2)]
    otmp = [R2[:, 17408 + b * 1024:17408 + (b + 1) * 1024].bitcast(F32) for b in range(2)]
    rd = [QST[:, b * 1024:(b + 1) * 1024].bitcast(F32) for b in range(2)]
    pastneg = v3(XT[:, 0:1024].bitcast(F32), pat="p (s v) -> p s v", v=32)
    ownsel = v3(XT[:, 1024:2048].bitcast(F32), pat="p (s v) -> p s v", v=32)
    gm = v3(XT[:, 2048:4096].bitcast(F32), pat="p (t v) -> p t v", v=32)
    sel = v3(XT[:, 4096:6144].bitcast(F32), pat="p (t v) -> p t v", v=32)
    nm = v3(XT[:, 6144:7168], pat="p (t v) -> p t v", v=32)
    top8 = v3(XT[:, 7168:7680].bitcast(F32), pat="p (t e) -> p t e", e=8)
    thr = XT[:, 7680:7744].bitcast(F32)
    KB = [R1[:, b * 8192:(b + 1) * 8192] for b in range(2)]
    VB = [v3(R1[:, 16384 + b * 4160:16384 + (b + 1) * 4160], pat="p (t c) -> p t c", c=65) for b in range(2)]
    QB = [R1[:, 24704 + b * 4096:24704 + (b + 1) * 4096] for b in range(2)]
    Wg = v3(R1[:, 0:16384], pat="p (k n) -> p k n", k=8)
    Woa = v3(R1[:, 16384:20480], pat="p (k n) -> p k n", k=4)
    Wob = v3(R1[:, 20480:24576], pat="p (k n) -> p k n", k=4)
    Wo = v3(R1[:, 24576:32768], pat="p (k n) -> p k n", k=8)
    MG = v3(R2[:, 0:4096], pat="p (j t) -> p j t", j=8)
    tmp4 = [R2[:, 4096 + k * 1024:4096 + (k + 1) * 1024].bitcast(F32) for k in range(4)]
    xr = [R2[:, 8192 + b * 2048:8192 + (b + 1) * 2048].bitcast(F32) for b in range(2)]
    yt = [R2[:, 12288 + b * 2048:12288 + (b + 1) * 2048].bitcast(F32) for b in range(2)]
    YT = [yt[0], yt[1], YTB[:, 0:1024], YTB[:, 1024:2048]]
    rdh = [YTB[:, b * 256:(b + 1) * 256].bitcast(BF16) for b in range(2)]
    rdl = [YTB[:, 512 + b * 256:512 + (b + 1) * 256].bitcast(BF16) for b in range(2)]
    gam = R2[:, 16384:18432].bitcast(F32)
    bet = R2[:, 18432:20480].bitcast(F32)

    xT_v = xT_d.ap().rearrange("(k p) t -> p k t", p=128)

    bank_free = [None] * 8

    def _finish():
        toks = [(P.psem[e], P.pcount[e]) for e in ("pe", "act", "dve", "pool") if P.pcount[e] > 0]
        toks += [(ds.h, ds.count) for ds in P.all_dsems if ds.count > 0]
        P.wait("sp", toks)
        P.emit()
        P.close()
        return nc

    ds_c = P.dsem("ds_const")
    t_c = None
    for (o, i_) in ((rel_sb[:, :], rel_d.ap()), (ohA_sb, ohA_d.ap()), (ohB_sb, ohB_d.ap()),
                    (mkA_sb, mkA_d.ap()), (mkB_sb, mkB_d.ap()), (bg_sb[:, :], bg_d.ap()),
                    (sink_sb[64:65, :], sinks_d.ap()), (firstneg[:, :], firstneg_d.ap())):
        t_c = DMA(P, "sp", o, i_, ds_c)
    ds_cb = P.dsem("ds_const_bf")
    DMA(P, "pool", ident[:, :], ident_d.ap(), ds_cb)
    t_cb = DMA(P, "pool", antiid[:, :], antiid_d.ap(), ds_cb)
    ds_w1 = P.dsem("ds_wkv")
    t_wkv = DMA(P, "pool", Wkv, wkv_d.ap().rearrange("(k p) n -> p k n", p=128), ds_w1)
    t_m1 = P.op("pool", lambda e: e.memset(ones_f[:, :], 0.5))
    P.op("pool", lambda e: e.memset(half_c[:, :], 0.5))
    t_hb = P.op("pool", lambda e: e.memset(halfs_bf[:, :], 0.5))
    P.op("pool", lambda e: e.memset(VA[:, :, :, 64:65], 1.0))
    P.op("pool", lambda e: e.memset(vst[0][:, :, :, 64:65], 1.0))
    P.op("pool", lambda e: e.memset(vst[1][:, :, :, 64:65], 1.0))
    t_mset = P.op("pool", lambda e: e.memset(ksum[:, :], 0.0))
    ds_w2 = P.dsem("ds_wown")
    t_nb = TS(P, "dve", hbg[:, :], bg_sb[:, :], 0.5, None, ALU.mult, waits=[t_c])
    t_es = ACTF(P, es_sb[64:65, :], sink_sb[64:65, :], AF.Exp, waits=[t_c])
    t_fv = None
    ds_fv = P.dsem("ds_fv")
    for (oh, mk, fsb, c0, b0, r0) in ((ohA_sb, mkA_sb, fsbA, 0, 4, 0), (ohB_sb, mkB_sb, fsbB, 8, 6, 8)):
        t_st = None
        for hf in range(2):
            tmm = MM(P, PS[b0 + hf][0:8, 0:384], rel_sb[0:32, c0:c0 + 8], oh[:, hf * 384:(hf + 1) * 384],
                     True, True, waits=[t_c], inc=True)
            t_st = STT(P, fsb[:, hf * 384:(hf + 1) * 384], PS[b0 + hf][0:8, 0:384], 8.0,
                       mk[:, hf * 384:(hf + 1) * 384], ALU.mult, ALU.add, waits=[tmm])
            bank_free[b0 + hf] = t_st
        t_fv = DMA(P, "sp", fv_s.ap()[r0:r0 + 8, :], fsb, ds_fv, waits=[t_st])
    t_setup_done = t_fv
    if stage == 0:
        return _finish()

    ds_xt = [P.dsem("ds_xt0"), P.dsem("ds_xt1")]
    ds_ksp = [P.dsem("ds_ksp0"), P.dsem("ds_ksp1")]
    ds_vsp = [P.dsem("ds_vsp0"), P.dsem("ds_vsp1")]
    pe_chunk_done = {}
    t_ksp = {}
    t_vsp = {}
    kb_v = kb_s.ap().rearrange("(a r) t -> r a t", r=128)
    vb_v = vb_s.ap().rearrange("h p t c -> p h t c")
    kcnt = 0
    vcnt = 0
    for c in range(16):
        b = c % 2
        t_x = DMA(P, "pool", xt[b], xT_v[:, :, c * 512:(c + 1) * 512], ds_xt[b],
                  waits=[pe_chunk_done.get(c - 2)])
        if c == 1:
            t_wown = DMA(P, "pool", Wown, wown_d.ap().rearrange("(k p) n -> p k n", p=128), ds_w2)
        t_lastK = None
        for g in range(5):
            bk = kcnt % 2
            kcnt += 1
            for kc in range(8):
                tmm = MM(P, PS[bk][:, :], Wkv[:, kc, g * 128:(g + 1) * 128], xt[b][:, kc, :], kc == 0, kc == 7,
                         waits=[t_x, t_wkv, bank_free[bk]] if kc == 0 else [], inc=(kc == 7))
            if g == 0:
                te = COPY(P, "dve", KA[:, c * 512:(c + 1) * 512], PS[bk][:, :], waits=[tmm, t_setup_done])
            else:
                COPY(P, "dve", kst[b][:, g - 1, :], PS[bk][:, :], waits=[tmm, t_ksp.get(c - 2)], inc=False)
                te = P.op("dve", lambda e, o=ksum[:, (g - 1) * 32 + 2 * c:(g - 1) * 32 + 2 * c + 2],
                          i=PS[bk][:, :].rearrange("p (a t) -> p a t", a=2):
                          e.tensor_reduce(out=o, in_=i, axis=AX.X, op=ALU.add), waits=[t_mset])
                t_lastK = te
            bank_free[bk] = te
        t_ksp[c] = DMA(P, "sp", kb_v[:, :, c * 512:(c + 1) * 512], kst[b], ds_ksp[b], waits=[t_lastK])
        t_lastV = None
        for t in range(4):
            bv = 2 + vcnt % 2
            ba = 4 + vcnt % 2
            vcnt += 1
            for kc in range(8):
                tmv = MM(P, PS[bv][:, :], xt[b][:, kc, t * 128:(t + 1) * 128], Wkv[:, kc, 768:1280], kc == 0, kc == 7,
                         waits=[bank_free[bv]] if kc == 0 else [], inc=(kc == 7))
            for kc in range(8):
                tma = MM(P, PS[ba][:, 0:128], xt[b][:, kc, t * 128:(t + 1) * 128], Wkv[:, kc, 640:768], kc == 0, kc == 7,
                         waits=[bank_free[ba]] if kc == 0 else [], inc=(kc == 7))
            te = COPY(P, "act", vst[b][:, :, t, 0:64], PS[bv][:, :].rearrange("p (h d) -> p h d", h=8),
                      waits=[tmv, t_vsp.get(c - 2)])
            bank_free[bv] = te
            te = COPY(P, "act", VA[:, c * 4 + t, :, 0:64], PS[ba][:, 0:128].rearrange("p (g d) -> p g d", g=2),
                      waits=[tma, t_setup_done])
            bank_free[ba] = te
            t_lastV = te
        pe_chunk_done[c] = tma
        t_vsp[c] = DMA(P, "sp", vb_v[:, :, c * 4:(c + 1) * 4, :], vst[b], ds_vsp[b], waits=[t_lastV])
    t_p1_pe = pe_chunk_done[15]
    kmv = kmean[:, :].rearrange("p (h v) -> p h v", v=32)
    ksv = ksum[:, :].rearrange("p (a v) -> p a v", v=32)
    kmv4 = kmean[:, :].rearrange("p (a two v) -> p a two v", two=2, v=32)
    TS(P, "dve", kmv4[0:64, :, 0, :], ksv[0:64, :, :], 1.0 / 256.0, None, ALU.mult, waits=[t_lastK])
    t_kmean = TS(P, "dve", kmv4[0:64, :, 1, :], ksv[64:128, :, :], 1.0 / 256.0, None, ALU.mult, waits=[t_lastK])

    if stage == 1:
        return _finish()

    ds_xq = [P.dsem("ds_xq0"), P.dsem("ds_xq1")]
    ds_qsp = [P.dsem("ds_qsp0"), P.dsem("ds_qsp1")]
    qb_v = qb_s.ap().rearrange("(a r) t -> r a t", r=128)
    pe_slot_done = {}
    t_qsp = {}
    t_e_free = [None, None]
    ecnt = 0
    bcnt = 0
    t_p1_spill = [t_ksp[14], t_ksp[15], t_vsp[14], t_vsp[15]]
    for i in range(NSLOT):
        b = i % 2
        w = [pe_slot_done.get(i - 2)] if i >= 2 else [pe_chunk_done[14 + b]]
        t_x = DMA(P, "pool", xq[b], xT_v[:, :, (2 * i + 1) * 256:(2 * i + 2) * 256], ds_xq[b], waits=w)
        tsl = slice(i * 256, (i + 1) * 256)
        t_lastq = None
        for bi in range(8):
            bk = bcnt % 4
            bcnt += 1
            for jj in range(2):
                j = 2 * bi + jj
                for kc in range(8):
                    tmm = MM(P, PS[bk][:, jj * 256:(jj + 1) * 256], Wown[:, kc, j * 128:(j + 1) * 128], xq[b][:, kc, :],
                             kc == 0, kc == 7,
                             waits=[t_x, t_wown, bank_free[bk]] if (kc == 0 and jj == 0) else [],
                             inc=(kc == 7 and jj == 1))
            pv = PS[bk][:, :].rearrange("p (a t) -> p a t", a=2)
            kind = bi // 2
            a0 = 2 * (bi % 2)
            if kind == 0:
                te = COPY(P, "dve", QA[:, a0:a0 + 2, tsl], pv, waits=[tmm, t_setup_done])
            elif kind == 2:
                te = COPY(P, "dve", qst[b][:, a0:a0 + 2, :], pv, waits=[tmm, t_qsp.get(i - 2)])
                t_lastq = te
            else:
                eb = ecnt % 2
                ecnt += 1
                Z = ZAv if kind == 1 else ZBv
                t3 = ACTF(P, etmp[eb], PS[bk][:, :], AF.Tanh, scale=0.5, waits=[tmm, t_e_free[eb]])
                te = STT(P, Z[:, a0:a0 + 2, tsl], etmp[eb].rearrange("p (a t) -> p a t", a=2), 1.0, pv, ALU.add, ALU.mult,
                         waits=[t3, t_p1_pe] + t_p1_spill)
                t_e_free[eb] = te
            bank_free[bk] = te
        pe_slot_done[i] = tmm
        t_qsp[i] = DMA(P, "sp", qb_v[:, :, tsl], qst[b], ds_qsp[b], waits=[t_lastq])
    t_p2_pe = pe_slot_done[NSLOT - 1]
    t_p2_last_evac = bank_free[(bcnt - 1) % 4]
    if stage == 2:
        return _finish()

    ds_h = P.dsem("ds_htab")
    t_h = None
    for mix in range(2):
        for h in range(8):
            for kind, off in ((0, 128), (1, 0), (2, 256)):
                src = bass.AP(fv_s, (mix * 8 + h) * FL + off, [[1, 128], [1, 256]])
                t_h = DMA(P, "sp", Htab[:, (mix * 8 + h) * 3 + kind, :], src, ds_h,
                          waits=[t_p2_pe, t_p2_last_evac, t_fv])
    ds_pn = P.dsem("ds_pastneg")

    if stage == 25:
        return _finish()

    st = {"rb": 0, "wb": 0, "bc_free": None, "rd_free": [None, None], "w_free": [None, None],
          "o_free": [None, None]}
    BCB = 6

    def stage_D(bankP, t_pv, sink_ap):
        rb = st["rb"] % 2
        st["rb"] += 1
        if sink_ap is not None:
            t1 = ACTF(P, rd[rb][64:65, 0:256], PS[bankP][64:65, 0:256], AF.Identity, bias=sink_ap, scale=1.0,
                      waits=[t_pv, st["rd_free"][rb], t_es])
            t2 = P.op("dve", lambda e, o=rd[rb][64:65, 0:256]: e.reciprocal(out=o, in_=o), waits=[t1])
        else:
            t2 = P.op("dve", lambda e, o=rd[rb][64:65, 0:256], i=PS[bankP][64:65, 0:256]: e.reciprocal(out=o, in_=i),
                      waits=[t_pv, st["rd_free"][rb]])
            t_hi = COPY(P, "dve", rdh[rb][64:65, 0:256], rd[rb][64:65, 0:256], waits=[t2, mrh_free[rb]])
            t2 = TT(P, "dve", rdl[rb][64:65, 0:256], rd[rb][64:65, 0:256], rdh[rb][64:65, 0:256], ALU.subtract, waits=[t_hi])
            st["rd_free"][rb] = t2
        return rb, t2

    mrh_free = [None, None]

    def stage_E(rb, t_rd):
        MM(P, PS[BCB][:, 0:256], halfs_bf[64:65, 0:128], rdh[rb][64:65, 0:256], True, False,
           waits=[t_rd, bank_free[BCB], t_hb])
        t = MM(P, PS[BCB][:, 0:256], halfs_bf[64:65, 0:128], rdl[rb][64:65, 0:256], False, True, inc=True)
        mrh_free[rb] = t
        return t

    def stage_F(bankP, t_bc, Zt, half, swa=False, t_pv=None):
        wb = st["wb"] % 2
        st["wb"] += 1
        rows = slice(half * 64, half * 64 + 64)
        if swa:
            t4a = COPY(P, "act", otmp[wb][rows, 0:256], PS[bankP][0:64, 0:256], waits=[t_pv, st["o_free"][wb]])
            bank_free[bankP] = t4a
            t3 = TT(P, "dve", wtmp[wb][rows, 0:256], Zt[rows, :], PS[BCB][rows, 0:256], ALU.mult,
                    waits=[t_bc, st["w_free"][wb]])
            bank_free[BCB] = t3
            t4 = TT(P, "pool", Zt[rows, :], otmp[wb][rows, 0:256], wtmp[wb][rows, 0:256], ALU.mult, waits=[t3, t4a])
            st["o_free"][wb] = t4
            st["w_free"][wb] = t4
            return t4
        t3 = TT(P, "dve", wtmp[wb][rows, 0:256], Zt[rows, :], PS[BCB][rows, 0:256], ALU.mult,
                waits=[t_bc, st["w_free"][wb]])
        bank_free[BCB] = t3
        if half == 0:
            t4 = TT(P, "dve", Zt[rows, :], PS[bankP][0:64, 0:256], wtmp[wb][rows, 0:256], ALU.mult, waits=[t3])
        else:
            t4a = COPY(P, "dve", otmp[wb][rows, 0:256], PS[bankP][0:64, 0:256], waits=[st["o_free"][wb]])
            t4 = TT(P, "dve", Zt[rows, :], otmp[wb][rows, 0:256], wtmp[wb][rows, 0:256], ALU.mult, waits=[t3, t4a])
            st["o_free"][wb] = t4
        st["w_free"][wb] = t4
        bank_free[bankP] = t4
        return t4

    KA1 = XT[:, 0:8192]
    w_ka = [t_p2_pe, t_p2_last_evac]
    P.op("pool", lambda e: e.memset(KA1[0:64, :], 0.0), waits=w_ka, inc=False)
    t_c1 = COPY(P, "pool", KA1[64:128, :], KA[64:128, :], waits=w_ka)
    t_kap = P.op("pool", lambda e: e.memset(KA[64:128, :], 0.0), waits=[t_c1])
    P.op("pool", lambda e: e.memset(rd[0][:, :], 1.0), waits=w_ka + [t_qsp[NSLOT - 2], t_qsp[NSLOT - 1]], inc=False)
    t_rdinit = P.op("pool", lambda e: e.memset(rd[1][:, :], 1.0), waits=w_ka)
    KAp = [KA, KA1]
    swa_rh_free = [None, None]
    if stage == 28:
        return _finish()
    units = [(i, m) for i in range(NSLOT) for m in range(4)]
    if stage in (29, 291, 292, 293):
        units = units[:6]
    pt_free = [None, None, None]
    sw = {}
    scnt = 0
    pcnt = 0
    NU = len(units)
    t_bc = None
    for n in range(NU + 2):
        if n < NU:
            i, m = units[n]
            g = m // 2
            pa = (2 * m) % 4
            rows = slice(g * 64, g * 64 + 64)
            tok0 = (2 * i + 1) * 256
            qs = i * 256
            sbk = scnt % 2
            pb = scnt % 3
            scnt += 1
            w0 = [bank_free[2 * sbk], bank_free[2 * sbk + 1], t_h, t_cb, t_p2_last_evac, t_kap]
            t_s = None
            for e_ in range(2):
                h = 2 * m + e_
                off = e_ * 512
                MM(P, PSB[sbk][:, off:off + 256], KAp[g][:, tok0:tok0 + 128], QA[:, pa + e_, qs:qs + 256], True, False,
                   waits=w0 if e_ == 0 else [])
                MM(P, PSB[sbk][:, off:off + 256], antiid[:, :], Htab[:, h * 3 + 0, :], False, True)
                MM(P, PSB[sbk][:, off + 256:off + 384], KAp[g][:, tok0 + 128:tok0 + 256], QA[:, pa + e_, qs + 128:qs + 256], True, False)
                MM(P, PSB[sbk][:, off + 256:off + 384], antiid[:, :], Htab[:, h * 3 + 1, 128:256], False, True)
                MM(P, PSB[sbk][:, off + 384:off + 512], KAp[g][:, tok0 - 128:tok0], QA[:, pa + e_, qs:qs + 128], True, False)
                t_s = MM(P, PSB[sbk][:, off + 384:off + 512], antiid[:, :], Htab[:, h * 3 + 2, 0:128], False, True, inc=(e_ == 1))
            if i == 0:
                ACTF(P, PT[pb][:, 0:384], PSB[sbk][:, 0:384], AF.Exp, scale=0.125, waits=[t_s, pt_free[pb]], inc=False)
                ACTF(P, PT[pb][:, 512:896], PSB[sbk][:, 512:896], AF.Exp, scale=0.125, inc=False)
                ACTF(P, PT[pb][:, 384:512], PSB[sbk][:, 384:512], AF.Exp, bias=firstneg[:, 0:1], scale=0.125, waits=[t_c], inc=False)
                t_e = ACTF(P, PT[pb][:, 896:1024], PSB[sbk][:, 896:1024], AF.Exp, bias=firstneg[:, 0:1], scale=0.125)
            else:
                t_e = ACTF(P, PT[pb][:, :], PSB[sbk][:, :], AF.Exp, scale=0.125, waits=[t_s, pt_free[pb]])
            bank_free[2 * sbk] = t_e
            bank_free[2 * sbk + 1] = t_e
            sw[n] = dict(pb=pb, t_e=t_e, i=i, m=m, g=g, pa=pa)
        if 1 <= n <= NU and stage != 291:
            u = sw[n - 1]
            i, m, g, pb = u["i"], u["m"], u["g"], u["pb"]
            T0 = (2 * i + 1) * 2
            bp = 4 + pcnt % 2
            pcnt += 1
            t_pv = None
            for e_ in range(2):
                c0 = e_ * 256
                o0 = e_ * 512
                MM(P, PS[bp][0:65, c0:c0 + 256], VA[:, T0, g, :], PT[pb][:, o0:o0 + 256], True, False,
                   waits=[u["t_e"], bank_free[bp]] if e_ == 0 else [])
                MM(P, PS[bp][0:65, c0 + 128:c0 + 256], VA[:, T0 + 1, g, :], PT[pb][:, o0 + 256:o0 + 384], False, False)
                t_pv = MM(P, PS[bp][0:65, c0:c0 + 128], VA[:, T0 - 1, g, :], PT[pb][:, o0 + 384:o0 + 512], False, True,
                          inc=(e_ == 1))
            pt_free[pb] = t_pv
            rb = st["rb"] % 2
            st["rb"] += 1
            ACTF(P, rd[rb][64:65, 0:256], PS[bp][64:65, 0:256], AF.Identity, bias=es_sb[64:65, 2 * m:2 * m + 1], scale=1.0,
                 waits=[t_pv, st["rd_free"][rb], t_es, t_rdinit], inc=False)
            t_d1 = ACTF(P, rd[rb][0:1, 0:256], PS[bp][64:65, 256:512], AF.Identity,
                        bias=es_sb[64:65, 2 * m + 1:2 * m + 2], scale=1.0)
            t_rc = P.op("dve", lambda e, o=rd[rb][0:65, 0:256]: e.reciprocal(out=o, in_=o), waits=[t_d1])
            t_mv = COPY(P, "act", rd[rb][64:65, 256:512], rd[rb][0:1, 0:256], waits=[t_rc])
            t_hi = COPY(P, "dve", rdh[rb][64:65, 0:512], rd[rb][64:65, 0:512], waits=[t_mv, swa_rh_free[rb]])
            t_rd = TT(P, "dve", rdl[rb][64:65, 0:512], rd[rb][64:65, 0:512], rdh[rb][64:65, 0:512], ALU.subtract, waits=[t_hi])
            st["rd_free"][rb] = t_rd
            u.update(bp=bp, rb=rb, t_rd=t_rd, t_pv=t_pv)
        if 2 <= n and stage not in (291, 292):
            u = sw[n - 2]
            rb, bp, g, pa, i = u["rb"], u["bp"], u["g"], u["pa"], u["i"]
            rows = slice(g * 64, g * 64 + 64)
            MM(P, PS[BCB][:, 0:512], halfs_bf[64:65, 0:128], rdh[rb][64:65, 0:512], True, False,
               waits=[u["t_rd"], bank_free[BCB], t_hb])
            t_bc = MM(P, PS[BCB][:, 0:512], halfs_bf[64:65, 0:128], rdl[rb][64:65, 0:512], False, True, inc=True)
            swa_rh_free[rb] = t_bc
            if stage == 293:
                bank_free[BCB] = None
                continue
            wb = st["wb"] % 2
            st["wb"] += 1
            Zp = ZAv[rows, pa:pa + 2, i * 256:(i + 1) * 256]
            t4a = COPY(P, "act", otmp[wb][rows, 0:512], PS[bp][0:64, 0:512], waits=[u["t_pv"], st["o_free"][wb]])
            bank_free[bp] = t4a
            t3 = TT(P, "dve", wtmp[wb][rows, 0:512].rearrange("p (a t) -> p a t", a=2), Zp,
                    PS[BCB][rows, 0:512].rearrange("p (a t) -> p a t", a=2), ALU.mult,
                    waits=[t_bc, st["w_free"][wb]])
            bank_free[BCB] = t3
            t4 = TT(P, "pool", Zp, otmp[wb][rows, 0:512].rearrange("p (a t) -> p a t", a=2),
                    wtmp[wb][rows, 0:512].rearrange("p (a t) -> p a t", a=2), ALU.mult, waits=[t3, t4a])
            st["o_free"][wb] = t4
            st["w_free"][wb] = t4
    t_swa_done_pe = t_bc
    t_swa_pool = (P.psem["pool"], P.pcount["pool"])
    if stage in (3, 29, 291, 292, 293):
        return _finish()

    DMA(P, "sp", XT[:, 0:1024].bitcast(F32), pastneg_d.ap(), ds_pn, waits=[t_swa_done_pe])
    t_pn = DMA(P, "sp", XT[:, 1024:2048].bitcast(F32), ownsel_d.ap(), ds_pn, waits=[t_swa_done_pe])
    ds_kb = [P.dsem("ds_kb0"), P.dsem("ds_kb1")]
    ds_vb = [P.dsem("ds_vb0"), P.dsem("ds_vb1")]
    ds_qb = [P.dsem("ds_qb0"), P.dsem("ds_qb1")]
    ds_bi = P.dsem("ds_blkind")
    t_bi = None
    for b in range(2):
        t_bi = DMA(P, "pool", KB[b][64:96, :], blkind_d.ap(), ds_bi, waits=[t_swa_done_pe])
    kb_h = kb_s.ap().rearrange("(h d) t -> h d t", d=64)
    qb_h = qb_s.ap().rearrange("(h d) t -> h d t", d=64)
    head_pe_done = {}
    head_loads = {}
    all_spills = [t_ksp[14], t_ksp[15], t_vsp[14], t_vsp[15], t_qsp[14], t_qsp[15]]

    def moba_loads(h):
        bb = h % 2
        w = [head_pe_done.get(h - 2), t_swa_done_pe] + all_spills
        t3 = DMA(P, "sp", QB[bb][0:64, :], qb_h[h], ds_qb[bb], waits=w)
        t1 = DMA(P, "sp", KB[bb][0:64, :], kb_h[h], ds_kb[bb], waits=w)
        t2 = DMA(P, "sp", VB[bb], vb_s.ap()[h], ds_vb[bb], waits=w)
        head_loads[h] = [t1, t2, t3]

    prep_done = {}

    def moba_prep(h):
        bb = h % 2
        lw = head_loads[h]
        kmh = kmean[0:64, h * 32:(h + 1) * 32]
        bk = 7
        tg = []
        for half in range(2):
            for t in range(16):
                tt = half * 16 + t
                tm = MM(P, PS[bk][:, t * 32:(t + 1) * 32], QB[bb][0:64, tt * 128:(tt + 1) * 128], kmh, True, True,
                        waits=[lw[2], t_kmean, bank_free[bk]] if t == 0 else [], inc=(t == 15))
            g = TT(P, "dve", gm[:, half * 16:(half + 1) * 16, :].rearrange("p (s two) v -> p s two v", two=2),
                   PS[bk][:, :].rearrange("p (s two v) -> p s two v", two=2, v=32),
                   pastneg[:, half * 8:(half + 1) * 8, :].unsqueeze(2).broadcast_to([128, 8, 2, 32]),
                   ALU.add, waits=[tm, t_pn, prep_done.get(h - 1)])
            bank_free[bk] = g
            tg.append(g)
        tmx = None
        for tt in range(32):
            tmx = P.op("dve", lambda e, o=top8[:, tt, :], i_=gm[:, tt, :]: e.max(out=o, in_=i_),
                       waits=tg if tt == 0 else [], inc=(tt == 31))
        t1 = TS(P, "dve", thr[:, 0:32], top8[:, :, 2], -1e8, None, ALU.max, waits=[tmx])
        t2 = TT(P, "dve", sel[:, :, :], gm[:, :, :], thr[:, 0:32].unsqueeze(2).broadcast_to([128, 32, 32]),
                ALU.is_ge, waits=[t1])
        t3 = TT(P, "dve", sel[:, :, :].rearrange("p (s two) v -> p s two v", two=2),
                sel[:, :, :].rearrange("p (s two) v -> p s two v", two=2),
                ownsel[:, :, :].unsqueeze(2).broadcast_to([128, 16, 2, 32]), ALU.max, waits=[t2])
        t4 = TS(P, "dve", nm[:, :, :], sel[:, :, :], -NEGM, NEGM, ALU.mult, ALU.add, waits=[t3])
        prep_t4[h] = t4

    prep_t4 = {}

    def moba_prep_b_step(h, g8):
        bb = h % 2
        bk = 7
        for q4 in range(4):
            tt = g8 * 4 + q4
            tm = MM(P, PS[bk][0:32, q4 * 128:(q4 + 1) * 128], nm[:, tt, :], ident[:, :], True, True,
                    waits=[prep_t4[h], t_cb, bank_free[bk]] if q4 == 0 else [], inc=(q4 == 3))
        tc = COPY(P, "dve", QB[bb][64:96, g8 * 512:(g8 + 1) * 512], PS[bk][0:32, :], waits=[tm])
        bank_free[bk] = tc
        if g8 == 7:
            prep_done[h] = tc

    def moba_prep_b(h):
        for g8 in range(8):
            moba_prep_b_step(h, g8)

    mst = {"u": 0, "pv": 0}

    def moba_attention(h, mid_hook):
        bb = h % 2
        half, p = h % 2, h // 2
        lw = head_loads[h] + [prep_done[h], t_bi, t_h, t_cb]
        ulist = [(i, w) for i in range(NSLOT) for w in range(i + 1)]
        pend = {}
        tails = []
        NUu = len(ulist)
        LAG = 2
        for n in range(NUu + LAG):
            if n < NUu:
                i, w = ulist[n]
                sb = mst["u"] % 2
                pb = mst["u"] % 3
                mst["u"] += 1
                qs = i * 256
                lastw = (w == i)
                first = True
                for vv in range(2):
                    v = 2 * w + vv
                    for a in range(2):
                        kt = v * 256 + a * 128
                        col = (vv * 2 + a) * 256
                        bkk = 2 * sb + vv
                        special = lastw and (vv == 1 or a == 1)
                        fin = (vv == 1 and a == 1)
                        MM(P, PSB[sb][:, col:col + 256], KB[bb][0:96, kt:kt + 128], QB[bb][0:96, qs:qs + 256],
                           True, not special, waits=(lw + [bank_free[2 * sb], bank_free[2 * sb + 1]]) if first else [],
                           inc=(fin and not special))
                        first = False
                        if special:
                            kind = a if vv == 1 else 2
                            MM(P, PSB[sb][:, col:col + 256], antiid[:, :], Htab[:, (8 + h) * 3 + kind, :],
                               False, True, inc=fin)
                t_s = (P.psem["pe"], P.pcount["pe"])
                t_e = ACTF(P, PT[pb][:, :], PSB[sb][:, :], AF.Exp, scale=0.125, waits=[t_s, pt_free[pb]])
                bank_free[2 * sb] = t_e
                bank_free[2 * sb + 1] = t_e
                pend[n] = dict(pb=pb, t_e=t_e)
            if n >= LAG:
                i, w = ulist[n - LAG]
                u = pend.pop(n - LAG)
                if w == 0:
                    mst["bp"] = 4 + mst["pv"] % 2
                    mst["pv"] += 1
                    while any(tl[2] == mst["bp"] for tl in tails):
                        tails.pop(0)[1]()
                bp = mst["bp"]
                lastw = (w == i)
                for vv in range(2):
                    v = 2 * w + vv
                    for a in range(2):
                        col = (vv * 2 + a) * 256
                        fin = (vv == 1 and a == 1)
                        t_pv = MM(P, PS[bp][0:65, 0:256], VB[bb][:, v * 2 + a, :], PT[u["pb"]][:, col:col + 256],
                                  (w == 0 and vv == 0 and a == 0), (lastw and fin),
                                  waits=[u["t_e"], bank_free[bp]] if (vv == 0 and a == 0) else [], inc=fin)
                pt_free[u["pb"]] = t_pv
                if lastw:
                    rb, t_rd = stage_D(bp, t_pv, None)
                    Zt = ZBv[:, p, i * 256:(i + 1) * 256]
                    tails.append([5, (lambda rb=rb, t_rd=t_rd, bp=bp, Zt=Zt: stage_F(bp, stage_E(rb, t_rd), Zt, half)), bp])
                    if mid_hook is not None:
                        mid_hook(i)
            for tl in tails:
                tl[0] -= 1
            while tails and tails[0][0] <= 0:
                tails.pop(0)[1]()
        for tl in tails:
            tl[1]()
        head_pe_done[h] = (P.psem["pe"], P.pcount["pe"])

    moba_loads(0)
    if stage == 31:
        return _finish()
    moba_prep(0)
    moba_prep_b(0)
    if stage == 32:
        return _finish()
    moba_loads(1)
    for h in range(8):
        def hook(i, h=h):
            if h + 1 < 8:
                if i == 4:
                    moba_prep(h + 1)
                if 7 <= i < 15:
                    moba_prep_b_step(h + 1, i - 7)
        use_hook = stage not in (33, 36)
        if not use_hook and h + 1 < 8 and stage == 36:
            moba_prep(h + 1)
            moba_prep_b(h + 1)
        moba_attention(h, hook if use_hook else None)
        if stage in (33, 34):
            return _finish()
        if stage == 36 and h == 1:
            return _finish()
        if stage >= 40 and stage < 48 and h == stage - 40:
            return _finish()
        if h + 2 < 8:
            moba_loads(h + 2)
    t_moba_pe = head_pe_done[7]
    t_moba_dve = (P.psem["dve"], P.pcount["dve"])

    if stage == 35:
        return _finish()

    p3_done = [t_moba_pe, t_moba_dve, t_swa_pool]
    ds_w4 = P.dsem("ds_w4")
    ds_w4_dummy = None
    ds_gb = P.dsem("ds_gb")
    DMA(P, "sp", gam, bass.AP(lng_d, 0, [[0, 128], [1, D]]), ds_gb, waits=p3_done)
    t_gb = DMA(P, "sp", bet, bass.AP(lnb_d, 0, [[0, 128], [1, D]]), ds_gb, waits=p3_done)
    ds_xr = [P.dsem("ds_xr0"), P.dsem("ds_xr1")]
    ds_out = [P.dsem(f"ds_out{k}") for k in range(4)]
    pe_g_done = {}
    t_out_done = None
    t_tmp_free = [None] * 4
    xr_free = [None, None]
    yt_free = [None] * 4
    out_toks = []
    lncnt = 0
    lnm = lnst[:, :].rearrange("p (t k) -> p t k", k=6)
    t_x4 = {}

    def load_x4(tc):
        b = tc % 2
        w = [pe_g_done.get(tc - 2)] + p3_done
        DMA(P, "pool", xt[b][:, :, 0:256], xT_v[:, :, (4 * tc + 1) * 256:(4 * tc + 2) * 256], ds_xt[b], waits=w)
        t_x4[tc] = DMA(P, "pool", xt[b][:, :, 256:512], xT_v[:, :, (4 * tc + 3) * 256:(4 * tc + 4) * 256], ds_xt[b], waits=w)

    pending_norm = []
    load_x4(0)
    ds_wj = [P.dsem(f"ds_wj{j}") for j in range(8)]
    t_wj = []
    for j in range(8):
        DMA(P, "pool", Wg[:, :, j * 128:(j + 1) * 128], wg_d.ap()[j], ds_wj[j], waits=p3_done)
        DMA(P, "pool", Wg[:, :, 1024 + j * 128:1024 + (j + 1) * 128], wg_d.ap()[8 + j], ds_wj[j], waits=p3_done)
        DMA(P, "pool", Woa[:, :, j * 128:(j + 1) * 128], woa_d.ap()[j], ds_wj[j], waits=p3_done)
        t_wj.append(DMA(P, "pool", Wob[:, :, j * 128:(j + 1) * 128], wob_d.ap()[j], ds_wj[j], waits=p3_done))
    t_w4 = DMA(P, "pool", Wo, wo_d.ap().rearrange("(k p) n -> p k n", p=128), ds_w4, waits=p3_done)
    for tc in range(8):
        b = tc % 2
        if tc + 1 < 8:
            load_x4(tc + 1)
        t_x = t_x4[tc]
        tsl = slice(tc * 512, (tc + 1) * 512)
        for j in range(8):
            bA, bB = j % 2, 2 + j % 2
            for kc in range(8):
                t_gA = MM(P, PS[bA][:, :], Wg[:, kc, j * 128:(j + 1) * 128], xt[b][:, kc, :], kc == 0, kc == 7,
                          waits=[t_x, t_wj[j], bank_free[bA]] if kc == 0 else [], inc=(kc == 7))
            for kc in range(8):
                t_gB = MM(P, PS[bB][:, :], Wg[:, kc, 1024 + j * 128:1024 + (j + 1) * 128], xt[b][:, kc, :], kc == 0, kc == 7,
                          waits=[bank_free[bB]] if kc == 0 else [], inc=(kc == 7))
            for k in range(4):
                t_uA = MM(P, PS[4][:, :], Woa[:, k, j * 128:(j + 1) * 128], ZAv[:, k, tsl], k == 0, k == 3,
                          waits=[bank_free[4]] if k == 0 else [], inc=(k == 3))
            for k in range(4):
                t_uB = MM(P, PS[5][:, :], Wob[:, k, j * 128:(j + 1) * 128], ZBv[:, k, tsl], k == 0, k == 3,
                          waits=[bank_free[5]] if k == 0 else [], inc=(k == 3))
            sig = []
            for (bk, tg, ti, col) in ((bA, t_gA, 0, j), (bB, t_gB, 1, 8 + j)):
                t1 = ACTF(P, tmp4[ti], PS[bk][:, :], AF.Tanh, bias=hbg[:, col:col + 1], scale=0.5,
                          waits=[tg, t_tmp_free[ti], t_nb] + p3_done)
                bank_free[bk] = t1
                t3 = ACTF(P, tmp4[ti], tmp4[ti], AF.Identity, bias=half_c[:, 0:1], scale=0.5, waits=[t1])
                sig.append(t3)
            tA = TT(P, "dve", tmp4[2], tmp4[0], PS[4][:, :], ALU.mult, waits=[sig[0], t_uA, t_tmp_free[2]] + p3_done)
            bank_free[4] = tA
            t_tmp_free[0] = tA
            tB = TT(P, "dve", tmp4[3], tmp4[1], PS[5][:, :], ALU.mult, waits=[sig[1], t_uB, t_tmp_free[3]])
            bank_free[5] = tB
            t_tmp_free[1] = tB
            tM = TT(P, "dve", MG[:, j, :], tmp4[2], tmp4[3], ALU.add, waits=[tA, tB, t_out_done])
            t_tmp_free[2] = tM
            t_tmp_free[3] = tM
            if pending_norm and 1 <= j <= 4:
                pending_norm.pop(0)()
        pe_g_done[tc] = t_uB
        t_ag = None
        for tt in range(4):
            r0 = tc * 512 + tt * 128
            xb = lncnt % 2
            lncnt += 1
            t_xr = DMA(P, "sp", xr[xb], xown_d.ap()[r0:r0 + 128, :], ds_xr[xb], waits=[xr_free[xb]] + p3_done)
            t_o = []
            for hf in range(2):
                bk = 6 + hf
                for jj in range(8):
                    t_mm = MM(P, PS[bk][:, :], MG[:, jj, tt * 128:(tt + 1) * 128], Wo[:, jj, hf * 512:(hf + 1) * 512],
                              jj == 0, jj == 7, waits=[tM, t_w4, bank_free[bk]] if jj == 0 else [], inc=(jj == 7))
                t_o.append(t_mm)
            t_out_done = t_o[1]
            ty = None
            for hf in range(2):
                cs = slice(hf * 512, (hf + 1) * 512)
                ty = STT(P, YT[tt][:, cs], xr[xb][:, cs], ALPHA, PS[6 + hf][:, :], ALU.mult, ALU.add,
                         waits=[t_o[hf], t_xr, yt_free[tt]])
                bank_free[6 + hf] = ty
            xr_free[xb] = ty
            P.op("dve", lambda e, o=lnst[:, 0:6], i_=YT[tt][:, 0:512]: e.bn_stats(out=o, in_=i_), waits=[ty, t_ag], inc=False)
            t_s2 = P.op("dve", lambda e, o=lnst[:, 6:12], i_=YT[tt][:, 512:1024]: e.bn_stats(out=o, in_=i_))
            t_ag = P.op("dve", lambda e, o=lnmv[:, 2 * tt:2 * tt + 2], i_=lnst[:, 0:12]: e.bn_aggr(out=o, in_=i_), waits=[t_s2])
        mv = lnmv[:, 0:8].rearrange("p (t k) -> p t k", k=2)
        t_ve = TS(P, "dve", lnmv[:, 8:12], mv[:, :, 1], LN_EPS, None, ALU.add, waits=[t_ag, yt_free[3]])
        t_l1 = ACTF(P, lnmv[:, 8:12], lnmv[:, 8:12], AF.Ln, waits=[t_ve])
        t_l2 = ACTF(P, lnmv[:, 8:12], lnmv[:, 8:12], AF.Exp, scale=-0.5, waits=[t_l1])
        def norm_store(tt, r0=None, t_l2=t_l2, tc=tc):
            r0 = tc * 512 + tt * 128
            t_n1 = STT(P, YT[tt][:, :], YT[tt][:, :], lnmv[:, 2 * tt:2 * tt + 1], gam, ALU.subtract, ALU.mult,
                       waits=[t_l2, t_gb])
            t_n2 = STT(P, YT[tt][:, :], YT[tt][:, :], lnmv[:, 8 + tt:9 + tt], bet, ALU.mult, ALU.add, waits=[t_n1])
            t_st = DMA(P, "sp", out_d.ap()[r0:r0 + 128, :], YT[tt], ds_out[tt], waits=[t_n2])
            yt_free[tt] = t_st
            out_toks.append(t_st)
        pending_norm[:] = [(lambda tt=tt, f=norm_store: f(tt)) for tt in range(4)]
    for f in pending_norm:
        f()
    P.wait("sp", out_toks[-4:])
    P.emit()
    P.close()
    return nc


def _t5_bucket_np(d):
    n = np.maximum(d, 0)
    nf = np.maximum(n, 1).astype(np.float32)
    large = 16 + (np.log(nf / np.float32(16)) / np.float32(math.log(128 / 16)) * np.float32(16)).astype(np.int32)
    large = np.minimum(large, 31)
    return np.where(n < 16, n, large)


def _constants(par):
    i = np.arange(FL)
    d = i - 255
    bkt = _t5_bucket_np(d)
    ohA = np.zeros((32, FL), np.float32)
    ohB = np.zeros((32, FL), np.float32)
    vA = (d >= 0) & (d <= 127)
    vB = d >= 0
    ohA[bkt[vA], i[vA]] = 1.0
    ohB[bkt[vB], i[vB]] += 1.0
    ohB[31, i[vB]] -= 1.0
    mkA = np.where(vA, 0.0, 8.0 * NEGM).astype(np.float32)[None].repeat(8, 0)
    mkB = np.where(vB, 0.0, 8.0 * NEGM).astype(np.float32)[None].repeat(8, 0)
    pastneg = np.full((NSLOT, 32), -1e9, np.float32)
    ownsel = np.zeros((NSLOT, 32), np.float32)
    for s in range(NSLOT):
        lo = 0 if par == 1 else 1
        pastneg[s, lo:2 * s + 1] = 0.0
        ownsel[s, 2 * s + 1] = 1.0
    pastneg = np.ascontiguousarray(np.broadcast_to(pastneg.reshape(1, -1), (128, NSLOT * 32)))
    ownsel = np.ascontiguousarray(np.broadcast_to(ownsel.reshape(1, -1), (128, NSLOT * 32)))
    firstneg = np.full((128, 1), NEGM if par == 0 else 0.0, np.float32)
    ident = np.eye(128, dtype=np.float32)
    antiid = np.ascontiguousarray(ident[::-1])
    blkind = np.zeros((32, SEQ), np.float32)
    for v in range(32):
        blkind[v, v * 256:(v + 1) * 256] = 1.0
    return dict(ohA=ohA, ohB=ohB, mkA=mkA, mkB=mkB, pastneg=pastneg, ownsel=ownsel, firstneg=firstneg,
                ident=ident, antiid=antiid, blkind=blkind)


def _weights(w_in, b_gate, sinks, rel_table, w_out_a, w_out_b, w_out, ln_gamma, ln_beta):
    w = np.asarray(w_in)[0]
    qA, kA, vA, zA = w[:, 0:512], w[:, 512:640], w[:, 640:768], w[:, 768:1280]
    qB, kB, vB, zB = w[:, 1280:1792], w[:, 1792:2304], w[:, 2304:2816], w[:, 2816:3328]
    gA, gB = w[:, 3328:4352], w[:, 4352:5376]

    def pair48(m):
        hm = m.reshape(m.shape[0], 8, 64)
        return np.concatenate([np.concatenate([hm[:, p], hm[:, p + 4]], axis=1) for p in range(4)], axis=1)

    w_kv = np.ascontiguousarray(np.concatenate([kA, kB, vA, vB], axis=1))
    w_own = np.ascontiguousarray(np.concatenate([pair48(qA), pair48(zA), qB, zB], axis=1))
    w_g = np.ascontiguousarray(np.concatenate([gA, gB], axis=1))
    woa = np.asarray(w_out_a)[0].reshape(8, 64, D)
    w_oa = np.ascontiguousarray(np.concatenate([np.concatenate([woa[p], woa[p + 4]], axis=0) for p in range(4)], axis=0))
    w_ob = np.ascontiguousarray(np.asarray(w_out_b)[0])
    w_o = np.ascontiguousarray(np.asarray(w_out)[0])
    bg = np.ascontiguousarray(np.asarray(b_gate)[0].reshape(16, 128).T)
    w_g = np.ascontiguousarray(w_g.reshape(8, 128, 16, 128).transpose(2, 1, 0, 3))
    w_oa = np.ascontiguousarray(w_oa.reshape(4, 128, 8, 128).transpose(2, 1, 0, 3))
    w_ob = np.ascontiguousarray(w_ob.reshape(4, 128, 8, 128).transpose(2, 1, 0, 3))
    return dict(w_kv=w_kv, w_own=w_own, w_g=w_g, w_oa=w_oa, w_ob=w_ob, w_o=w_o, bg=bg,
                rel=np.ascontiguousarray(np.asarray(rel_table), dtype=np.float32),
                sinks=np.ascontiguousarray(np.asarray(sinks)[0].reshape(1, 8)),
                lng=np.ascontiguousarray(np.asarray(ln_gamma)[0].reshape(1, D)),
                lnb=np.ascontiguousarray(np.asarray(ln_beta)[0].reshape(1, D)))


_NC_CACHE = {}


def kernel(x, w_in, b_gate, sinks, rel_table, w_out_a, w_out_b, w_out, ln_gamma, ln_beta):
    x = np.asarray(x, dtype=np.float32)
    wd = _weights(w_in, b_gate, sinks, rel_table, w_out_a, w_out_b, w_out, ln_gamma, ln_beta)
    wd = {k: np.asarray(v, dtype=np.float32) for k, v in wd.items()}
    in_maps = []
    for c in range(NCORES):
        b, par = c // 2, c % 2
        xb = x[b]
        if par == 1:
            virt = xb
        else:
            virt = np.concatenate([np.zeros((256, D), np.float32), xb[:SEQ - 256]], axis=0)
        xT = np.ascontiguousarray(virt.T)
        xown = np.ascontiguousarray(virt.reshape(32, 256, D)[1::2].reshape(NSLOT * 256, D))
        m = dict(xT=xT, xown=xown)
        m.update(wd)
        m.update(_constants(par))
        in_maps.append(m)
    if "nc" not in _NC_CACHE:
        _NC_CACHE["nc"] = build_nc()
    nc = _NC_CACHE["nc"]
    res = run_bass_kernel_spmd(nc, in_maps, core_ids=list(range(NCORES)))
    out = np.empty((4, SEQ, D), np.float32)
    for c in range(NCORES):
        b, par = c // 2, c % 2
        o = np.asarray(res.results[c]["out"]).reshape(NSLOT, 256, D)
        out[b].reshape(32, 256, D)[par::2] = o
    return out
```

```python
import math
from contextlib import ExitStack

import numpy as np
import concourse.bass as bass
import concourse.mybir as mybir
from concourse.bass_utils import run_bass_kernel_spmd

F32 = mybir.dt.float32
BF16 = mybir.dt.bfloat16
AF = mybir.ActivationFunctionType
ALU = mybir.AluOpType
AX = mybir.AxisListType

D = 1024
SEQ = 8192
NSLOT = 16
NCORES = 8
ALPHA = 2.0 ** 0.25
LN_EPS = 1e-5
NEGM = -30000.0
FL = 768


class DSem:
    def __init__(self, prog, name):
        self.h = prog.stack.enter_context(prog.nc.semaphore(name))
        self.count = 0


class _Dummy:
    def then_inc(self, *a):
        pass


class Prog:
    ENG = ("pe", "act", "dve", "pool", "sp")

    def __init__(self, nc):
        self.nc = nc
        self.stack = ExitStack()
        self.ops = {e: [] for e in self.ENG}
        self.psem = {e: self.stack.enter_context(nc.semaphore("prog_" + e)) for e in self.ENG}
        self.pcount = {e: 0 for e in self.ENG}
        self.nd = 0
        self.all_dsems = []

    def dsem(self, name=None):
        self.nd += 1
        d = DSem(self, name or f"dsem{self.nd}")
        self.all_dsems.append(d)
        return d

    def sb(self, name, shape, dt):
        return self.stack.enter_context(self.nc.sbuf_tensor(name, shape, dt))

    def ps(self, name, shape, dt):
        return self.stack.enter_context(self.nc.psum_tensor(name, shape, dt))

    def op(self, eng, fn, waits=(), inc=True):
        tok = None
        if inc:
            self.pcount[eng] += 1
            tok = (self.psem[eng], self.pcount[eng])
        self.ops[eng].append((fn, [w for w in waits if w is not None], (self.psem[eng], 1) if inc else None))
        return tok

    def dma(self, eng, fn, dsem, waits=()):
        dsem.count += 16
        tok = (dsem.h, dsem.count)
        self.ops[eng].append((fn, [w for w in waits if w is not None], (dsem.h, 16)))
        return tok

    def wait(self, eng, waits):
        self.ops[eng].append((None, [w for w in waits if w is not None], None))

    def emit(self):
        nc = self.nc
        with nc.Block() as block:
            def replay(engname):
                def run(e):
                    waited = {}
                    for fn, waits, inc in self.ops[engname]:
                        for (s, v) in waits:
                            k = id(s)
                            if waited.get(k, 0) >= v:
                                continue
                            waited[k] = v
                            e.wait_ge(s, v)
                        if fn is None:
                            continue
                        ins = fn(e)
                        if inc is not None:
                            ins.then_inc(inc[0], inc[1])
                return run
            block.tensor(replay("pe"))
            block.scalar(replay("act"))
            block.vector(replay("dve"))
            block.gpsimd(replay("pool"))
            block.sync(replay("sp"))

    def close(self):
        self.stack.close()


def MM(P, out, lhsT, rhs, start, stop, waits=(), inc=False):
    return P.op("pe", lambda e: e.matmul(out, lhsT=lhsT, rhs=rhs, start=start, stop=stop,
                                         skip_group_check=True), waits=waits, inc=inc)


def ACTF(P, out, in_, func, waits=(), bias=None, scale=1.0, inc=True):
    if bias is None:
        return P.op("act", lambda e: e.activation(out=out, in_=in_, func=func, scale=scale), waits=waits, inc=inc)
    return P.op("act", lambda e: e.activation(out=out, in_=in_, func=func, bias=bias, scale=scale),
                waits=waits, inc=inc)


def COPY(P, eng, out, in_, waits=(), inc=True):
    if eng == "act":
        return P.op("act", lambda e: e.copy(out=out, in_=in_), waits=waits, inc=inc)
    return P.op(eng, lambda e: e.tensor_copy(out=out, in_=in_), waits=waits, inc=inc)


def TT(P, eng, out, in0, in1, op, waits=(), inc=True):
    return P.op(eng, lambda e: e.tensor_tensor(out=out, in0=in0, in1=in1, op=op), waits=waits, inc=inc)


def TS(P, eng, out, in0, s1, s2, op0, op1=None, waits=(), inc=True):
    if op1 is None:
        return P.op(eng, lambda e: e.tensor_scalar(out=out, in0=in0, scalar1=s1, scalar2=None, op0=op0),
                    waits=waits, inc=inc)
    return P.op(eng, lambda e: e.tensor_scalar(out=out, in0=in0, scalar1=s1, scalar2=s2, op0=op0, op1=op1),
                waits=waits, inc=inc)


def STT(P, out, in0, scalar, in1, op0, op1, waits=(), inc=True):
    return P.op("dve", lambda e: e.scalar_tensor_tensor(out=out, in0=in0, scalar=scalar, in1=in1, op0=op0, op1=op1),
                waits=waits, inc=inc)


def DMA(P, eng, out, in_, dsem, waits=()):
    return P.dma(eng, lambda e: e.dma_start(out=out, in_=in_), dsem, waits=waits)


def build_nc(stage=99):
    nc = bass.Bass("TRN2", target_bir_lowering=False)
    dt_in = lambda n, s, d=F32: nc.dram_tensor(n, s, d, kind="ExternalInput")
    xT_d = dt_in("xT", [D, SEQ])
    xown_d = dt_in("xown", [NSLOT * 256, D])
    wkv_d = dt_in("w_kv", [D, 1280])
    wown_d = dt_in("w_own", [D, 2048])
    wg_d = dt_in("w_g", [16, 128, 8, 128])
    woa_d = dt_in("w_oa", [8, 128, 4, 128])
    wob_d = dt_in("w_ob", [8, 128, 4, 128])
    wo_d = dt_in("w_o", [D, D])
    bg_d = dt_in("bg", [128, 16])
    rel_d = dt_in("rel", [32, 16])
    sinks_d = dt_in("sinks", [1, 8])
    lng_d = dt_in("lng", [1, D])
    lnb_d = dt_in("lnb", [1, D])
    ohA_d = dt_in("ohA", [32, FL])
    ohB_d = dt_in("ohB", [32, FL])
    mkA_d = dt_in("mkA", [8, FL])
    mkB_d = dt_in("mkB", [8, FL])
    pastneg_d = dt_in("pastneg", [128, 512])
    ownsel_d = dt_in("ownsel", [128, 512])
    firstneg_d = dt_in("firstneg", [128, 1])
    ident_d = dt_in("ident", [128, 128])
    antiid_d = dt_in("antiid", [128, 128])
    blkind_d = dt_in("blkind", [32, SEQ])
    out_d = nc.dram_tensor("out", [NSLOT * 256, D], F32, kind="ExternalOutput")
    kb_s = nc.dram_tensor("kb_s", [512, SEQ], BF16, kind="Internal")
    vb_s = nc.dram_tensor("vb_s", [8, 128, 64, 65], BF16, kind="Internal")
    qb_s = nc.dram_tensor("qb_s", [512, NSLOT * 256], BF16, kind="Internal")
    fv_s = nc.dram_tensor("fv_s", [16, FL], BF16, kind="Internal")

    P = Prog(nc)
    ZA = P.sb("ZA", [128, 16384], BF16)
    ZB = P.sb("ZB", [128, 16384], BF16)
    R1 = P.sb("R1", [128, 33792], BF16)
    R2 = P.sb("R2", [128, 20480], BF16)
    XT = P.sb("XT", [128, 8192], BF16)
    QST = P.sb("QST", [128, 2048], BF16)
    ident = P.sb("ident_bf", [128, 128], BF16)
    antiid = P.sb("antiid_bf", [128, 128], BF16)
    ones_f = P.sb("ones_f", [128, 128], F32)
    rel_sb = P.sb("rel_sb", [32, 16], F32)
    bg_sb = P.sb("bg_sb", [128, 16], F32)
    negbg = P.sb("negbg", [128, 16], F32)
    sink_sb = P.sb("sink_sb", [128, 8], F32)
    es_sb = P.sb("es_sb", [128, 8], F32)
    firstneg = P.sb("firstneg_sb", [128, 1], F32)
    ksum = P.sb("ksum", [128, 4 * 32], F32)
    kmean = P.sb("kmean", [64, 8 * 32], BF16)
    lnst = P.sb("lnst", [128, 2 * 12], F32)
    lnmv = P.sb("lnmv", [128, 12], F32)
    YTB = P.sb("YTB", [128, 2048], F32)
    PSB = [P.ps(f"psb{i}", [128, 1024], F32) for i in range(4)]
    PS = [PSB[i // 2][:, (i % 2) * 512:(i % 2 + 1) * 512] for i in range(8)]
    half_c = P.sb("half_c", [128, 1], F32)
    halfs_bf = P.sb("halfs_bf", [128, 128], BF16)
    hbg = P.sb("hbg", [128, 16], F32)

    def v3(ap, **kw):
        pat = kw.pop("pat")
        return ap.rearrange(pat, **kw)

    Wkv = v3(ZA[:, 0:10240], pat="p (k n) -> p k n", k=8)
    kst = [v3(ZB[:, b * 2048:(b + 1) * 2048], pat="p (a t) -> p a t", a=4) for b in range(2)]
    vst = [v3(ZB[:, 4096 + b * 2080:4096 + (b + 1) * 2080], pat="p (h t c) -> p h t c", h=8, t=4) for b in range(2)]
    KA = R1[:, 0:8192]
    VA = v3(R1[:, 8192:8192 + 8320], pat="p (t g c) -> p t g c", g=2, c=65)
    QA = v3(R1[:, 16512:16512 + 16384], pat="p (a t) -> p a t", a=4)
    Wown = v3(R2[:, 0:16384], pat="p (k n) -> p k n", k=8)
    etmp = [R2[:, 16384 + b * 1024:16384 + (b + 1) * 1024].bitcast(F32) for b in range(2)]
    xt = [v3(XT[:, b * 4096:(b + 1) * 4096], pat="p (k t) -> p k t", k=8) for b in range(2)]
    xq = [v3(XT[:, b * 4096:b * 4096 + 2048], pat="p (k t) -> p k t", k=8) for b in range(2)]
    qst = [v3(QST[:, b * 1024:(b + 1) * 1024], pat="p (a t) -> p a t", a=4) for b in range(2)]
    ZAv = v3(ZA[:, :], pat="p (a t) -> p a t", a=4)
    ZBv = v3(ZB[:, :], pat="p (a t) -> p a t", a=4)
    S0 = 16512
    ohA_sb = R1[0:32, S0:S0 + 2 * FL].bitcast(F32)
    ohB_sb = R1[0:32, S0 + 2 * FL:S0 + 4 * FL].bitcast(F32)
    mkA_sb = R1[0:8, S0 + 4 * FL:S0 + 6 * FL].bitcast(F32)
    mkB_sb = R1[0:8, S0 + 6 * FL:S0 + 8 * FL].bitcast(F32)
    fsbA = R1[0:8, S0 + 8 * FL:S0 + 9 * FL]
    fsbB = R1[0:8, S0 + 9 * FL:S0 + 10 * FL]
    Htab = v3(R2[:, 0:12288], pat="p (n q) -> p n q", q=256)
    PT = [R2[:, 12288 + b * 1024:12288 + (b + 1) * 1024] for b in range(3)]
    wtmp = [R2[:, 15360 + b * 1024:15360 + (b + 1) * 1024].bitcast(F32) for b in range(2)]
    otmp = [R2[:, 17408 + b * 1024:17408 + (b + 1) * 1024].bitcast(F32) for b in range(2)]
    rd = [QST[:, b * 1024:(b + 1) * 1024].bitcast(F32) for b in range(2)]
    pastneg = v3(XT[:, 0:1024].bitcast(F32), pat="p (s v) -> p s v", v=32)
    ownsel = v3(XT[:, 1024:2048].bitcast(F32), pat="p (s v) -> p s v", v=32)
    gm = v3(XT[:, 2048:4096].bitcast(F32), pat="p (t v) -> p t v", v=32)
    sel = v3(XT[:, 4096:6144].bitcast(F32), pat="p (t v) -> p t v", v=32)
    nm = v3(XT[:, 6144:7168], pat="p (t v) -> p t v", v=32)
    top8 = v3(XT[:, 7168:7680].bitcast(F32), pat="p (t e) -> p t e", e=8)
    thr = XT[:, 7680:7744].bitcast(F32)
    KB = [R1[:, b * 8192:(b + 1) * 8192] for b in range(2)]
    VB = [v3(R1[:, 16384 + b * 4160:16384 + (b + 1) * 4160], pat="p (t c) -> p t c", c=65) for b in range(2)]
    QB = [R1[:, 24704 + b * 4096:24704 + (b + 1) * 4096] for b in range(2)]
    Wg = v3(R1[:, 0:16384], pat="p (k n) -> p k n", k=8)
    Woa = v3(R1[:, 16384:20480], pat="p (k n) -> p k n", k=4)
    Wob = v3(R1[:, 20480:24576], pat="p (k n) -> p k n", k=4)
    Wo = v3(R1[:, 24576:32768], pat="p (k n) -> p k n", k=8)
    MG = v3(R2[:, 0:4096], pat="p (j t) -> p j t", j=8)
    tmp4 = [R2[:, 4096 + k * 1024:4096 + (k + 1) * 1024].bitcast(F32) for k in range(4)]
    xr = [R2[:, 8192 + b * 2048:8192 + (b + 1) * 2048].bitcast(F32) for b in range(2)]
    yt = [R2[:, 12288 + b * 2048:12288 + (b + 1) * 2048].bitcast(F32) for b in range(2)]
    YT = [yt[0], yt[1], YTB[:, 0:1024], YTB[:, 1024:2048]]
    rdh = [YTB[:, b * 256:(b + 1) * 256].bitcast(BF16) for b in range(2)]
    rdl = [YTB[:, 512 + b * 256:512 + (b + 1) * 256].bitcast(BF16) for b in range(2)]
    gam = R2[:, 16384:18432].bitcast(F32)
    bet = R2[:, 18432:20480].bitcast(F32)

    xT_v = xT_d.ap().rearrange("(k p) t -> p k t", p=128)

    bank_free = [None] * 8

    def _finish():
        toks = [(P.psem[e], P.pcount[e]) for e in ("pe", "act", "dve", "pool") if P.pcount[e] > 0]
        toks += [(ds.h, ds.count) for ds in P.all_dsems if ds.count > 0]
        P.wait("sp", toks)
        P.emit()
        P.close()
        return nc

    ds_c = P.dsem("ds_const")
    t_c = None
    for (o, i_) in ((rel_sb[:, :], rel_d.ap()), (ohA_sb, ohA_d.ap()), (ohB_sb, ohB_d.ap()),
                    (mkA_sb, mkA_d.ap()), (mkB_sb, mkB_d.ap()), (bg_sb[:, :], bg_d.ap()),
                    (sink_sb[64:65, :], sinks_d.ap()), (firstneg[:, :], firstneg_d.ap())):
        t_c = DMA(P, "sp", o, i_, ds_c)
    ds_cb = P.dsem("ds_const_bf")
    DMA(P, "pool", ident[:, :], ident_d.ap(), ds_cb)
    t_cb = DMA(P, "pool", antiid[:, :], antiid_d.ap(), ds_cb)
    ds_w1 = P.dsem("ds_wkv")
    t_wkv = DMA(P, "pool", Wkv, wkv_d.ap().rearrange("(k p) n -> p k n", p=128), ds_w1)
    t_m1 = P.op("pool", lambda e: e.memset(ones_f[:, :], 0.5))
    P.op("pool", lambda e: e.memset(half_c[:, :], 0.5))
    t_hb = P.op("pool", lambda e: e.memset(halfs_bf[:, :], 0.5))
    P.op("pool", lambda e: e.memset(VA[:, :, :, 64:65], 1.0))
    P.op("pool", lambda e: e.memset(vst[0][:, :, :, 64:65], 1.0))
    P.op("pool", lambda e: e.memset(vst[1][:, :, :, 64:65], 1.0))
    t_mset = P.op("pool", lambda e: e.memset(ksum[:, :], 0.0))
    ds_w2 = P.dsem("ds_wown")
    t_nb = TS(P, "dve", hbg[:, :], bg_sb[:, :], 0.5, None, ALU.mult, waits=[t_c])
    t_es = ACTF(P, es_sb[64:65, :], sink_sb[64:65, :], AF.Exp, waits=[t_c])
    t_fv = None
    ds_fv = P.dsem("ds_fv")
    for (oh, mk, fsb, c0, b0, r0) in ((ohA_sb, mkA_sb, fsbA, 0, 4, 0), (ohB_sb, mkB_sb, fsbB, 8, 6, 8)):
        t_st = None
        for hf in range(2):
            tmm = MM(P, PS[b0 + hf][0:8, 0:384], rel_sb[0:32, c0:c0 + 8], oh[:, hf * 384:(hf + 1) * 384],
                     True, True, waits=[t_c], inc=True)
            t_st = STT(P, fsb[:, hf * 384:(hf + 1) * 384], PS[b0 + hf][0:8, 0:384], 8.0,
                       mk[:, hf * 384:(hf + 1) * 384], ALU.mult, ALU.add, waits=[tmm])
            bank_free[b0 + hf] = t_st
        t_fv = DMA(P, "sp", fv_s.ap()[r0:r0 + 8, :], fsb, ds_fv, waits=[t_st])
    t_setup_done = t_fv
    if stage == 0:
        return _finish()

    ds_xt = [P.dsem("ds_xt0"), P.dsem("ds_xt1")]
    ds_ksp = [P.dsem("ds_ksp0"), P.dsem("ds_ksp1")]
    ds_vsp = [P.dsem("ds_vsp0"), P.dsem("ds_vsp1")]
    pe_chunk_done = {}
    t_ksp = {}
    t_vsp = {}
    kb_v = kb_s.ap().rearrange("(a r) t -> r a t", r=128)
    vb_v = vb_s.ap().rearrange("h p t c -> p h t c")
    kcnt = 0
    vcnt = 0
    for c in range(16):
        b = c % 2
        t_x = DMA(P, "pool", xt[b], xT_v[:, :, c * 512:(c + 1) * 512], ds_xt[b],
                  waits=[pe_chunk_done.get(c - 2)])
        if c == 1:
            t_wown = DMA(P, "pool", Wown, wown_d.ap().rearrange("(k p) n -> p k n", p=128), ds_w2)
        t_lastK = None
        for g in range(5):
            bk = kcnt % 2
            kcnt += 1
            for kc in range(8):
                tmm = MM(P, PS[bk][:, :], Wkv[:, kc, g * 128:(g + 1) * 128], xt[b][:, kc, :], kc == 0, kc == 7,
                         waits=[t_x, t_wkv, bank_free[bk]] if kc == 0 else [], inc=(kc == 7))
            if g == 0:
                te = COPY(P, "dve", KA[:, c * 512:(c + 1) * 512], PS[bk][:, :], waits=[tmm, t_setup_done])
            else:
                COPY(P, "dve", kst[b][:, g - 1, :], PS[bk][:, :], waits=[tmm, t_ksp.get(c - 2)], inc=False)
                te = P.op("dve", lambda e, o=ksum[:, (g - 1) * 32 + 2 * c:(g - 1) * 32 + 2 * c + 2],
                          i=PS[bk][:, :].rearrange("p (a t) -> p a t", a=2):
                          e.tensor_reduce(out=o, in_=i, axis=AX.X, op=ALU.add), waits=[t_mset])
                t_lastK = te
            bank_free[bk] = te
        t_ksp[c] = DMA(P, "sp", kb_v[:, :, c * 512:(c + 1) * 512], kst[b], ds_ksp[b], waits=[t_lastK])
        t_lastV = None
        for t in range(4):
            bv = 2 + vcnt % 2
            ba = 4 + vcnt % 2
            vcnt += 1
            for kc in range(8):
                tmv = MM(P, PS[bv][:, :], xt[b][:, kc, t * 128:(t + 1) * 128], Wkv[:, kc, 768:1280], kc == 0, kc == 7,
                         waits=[bank_free[bv]] if kc == 0 else [], inc=(kc == 7))
            for kc in range(8):
                tma = MM(P, PS[ba][:, 0:128], xt[b][:, kc, t * 128:(t + 1) * 128], Wkv[:, kc, 640:768], kc == 0, kc == 7,
                         waits=[bank_free[ba]] if kc == 0 else [], inc=(kc == 7))
            te = COPY(P, "act", vst[b][:, :, t, 0:64], PS[bv][:, :].rearrange("p (h d) -> p h d", h=8),
                      waits=[tmv, t_vsp.get(c - 2)])
            bank_free[bv] = te
            te = COPY(P, "act", VA[:, c * 4 + t, :, 0:64], PS[ba][:, 0:128].rearrange("p (g d) -> p g d", g=2),
                      waits=[tma, t_setup_done])
            bank_free[ba] = te
            t_lastV = te
        pe_chunk_done[c] = tma
        t_vsp[c] = DMA(P, "sp", vb_v[:, :, c * 4:(c + 1) * 4, :], vst[b], ds_vsp[b], waits=[t_lastV])
    t_p1_pe = pe_chunk_done[15]
    kmv = kmean[:, :].rearrange("p (h v) -> p h v", v=32)
    ksv = ksum[:, :].rearrange("p (a v) -> p a v", v=32)
    kmv4 = kmean[:, :].rearrange("p (a two v) -> p a two v", two=2, v=32)
    TS(P, "dve", kmv4[0:64, :, 0, :], ksv[0:64, :, :], 1.0 / 256.0, None, ALU.mult, waits=[t_lastK])
    t_kmean = TS(P, "dve", kmv4[0:64, :, 1, :], ksv[64:128, :, :], 1.0 / 256.0, None, ALU.mult, waits=[t_lastK])

    if stage == 1:
        return _finish()

    ds_xq = [P.dsem("ds_xq0"), P.dsem("ds_xq1")]
    ds_qsp = [P.dsem("ds_qsp0"), P.dsem("ds_qsp1")]
    qb_v = qb_s.ap().rearrange("(a r) t -> r a t", r=128)
    pe_slot_done = {}
    t_qsp = {}
    t_e_free = [None, None]
    ecnt = 0
    bcnt = 0
    t_p1_spill = [t_ksp[14], t_ksp[15], t_vsp[14], t_vsp[15]]
    for i in range(NSLOT):
        b = i % 2
        w = [pe_slot_done.get(i - 2)] if i >= 2 else [pe_chunk_done[14 + b]]
        t_x = DMA(P, "pool", xq[b], xT_v[:, :, (2 * i + 1) * 256:(2 * i + 2) * 256], ds_xq[b], waits=w)
        tsl = slice(i * 256, (i + 1) * 256)
        t_lastq = None
        for bi in range(8):
            bk = bcnt % 4
            bcnt += 1
            for jj in range(2):
                j = 2 * bi + jj
                for kc in range(8):
                    tmm = MM(P, PS[bk][:, jj * 256:(jj + 1) * 256], Wown[:, kc, j * 128:(j + 1) * 128], xq[b][:, kc, :],
                             kc == 0, kc == 7,
                             waits=[t_x, t_wown, bank_free[bk]] if (kc == 0 and jj == 0) else [],
                             inc=(kc == 7 and jj == 1))
            pv = PS[bk][:, :].rearrange("p (a t) -> p a t", a=2)
            kind = bi // 2
            a0 = 2 * (bi % 2)
            if kind == 0:
                te = COPY(P, "dve", QA[:, a0:a0 + 2, tsl], pv, waits=[tmm, t_setup_done])
            elif kind == 2:
                te = COPY(P, "dve", qst[b][:, a0:a0 + 2, :], pv, waits=[tmm, t_qsp.get(i - 2)])
                t_lastq = te
            else:
                eb = ecnt % 2
                ecnt += 1
                Z = ZAv if kind == 1 else ZBv
                t3 = ACTF(P, etmp[eb], PS[bk][:, :], AF.Tanh, scale=0.5, waits=[tmm, t_e_free[eb]])
                te = STT(P, Z[:, a0:a0 + 2, tsl], etmp[eb].rearrange("p (a t) -> p a t", a=2), 1.0, pv, ALU.add, ALU.mult,
                         waits=[t3, t_p1_pe] + t_p1_spill)
                t_e_free[eb] = te
            bank_free[bk] = te
        pe_slot_done[i] = tmm
        t_qsp[i] = DMA(P, "sp", qb_v[:, :, tsl], qst[b], ds_qsp[b], waits=[t_lastq])
    t_p2_pe = pe_slot_done[NSLOT - 1]
    t_p2_last_evac = bank_free[(bcnt - 1) % 4]
    if stage == 2:
        return _finish()

    ds_h = P.dsem("ds_htab")
    t_h = None
    for mix in range(2):
        for h in range(8):
            for kind, off in ((0, 128), (1, 0), (2, 256)):
                src = bass.AP(fv_s, (mix * 8 + h) * FL + off, [[1, 128], [1, 256]])
                t_h = DMA(P, "sp", Htab[:, (mix * 8 + h) * 3 + kind, :], src, ds_h,
                          waits=[t_p2_pe, t_p2_last_evac, t_fv])
    ds_pn = P.dsem("ds_pastneg")

    if stage == 25:
        return _finish()

    st = {"rb": 0, "wb": 0, "bc_free": None, "rd_free": [None, None], "w_free": [None, None],
          "o_free": [None, None]}
    BCB = 7

    def stage_D(bankP, t_pv, sink_ap):
        rb = st["rb"] % 2
        st["rb"] += 1
        if sink_ap is not None:
            t1 = ACTF(P, rd[rb][64:65, 0:256], PS[bankP][64:65, 0:256], AF.Identity, bias=sink_ap, scale=1.0,
                      waits=[t_pv, st["rd_free"][rb], t_es])
            t2 = P.op("dve", lambda e, o=rd[rb][64:65, 0:256]: e.reciprocal(out=o, in_=o), waits=[t1])
        else:
            t2 = P.op("dve", lambda e, o=rd[rb][64:65, 0:256], i=PS[bankP][64:65, 0:256]: e.reciprocal(out=o, in_=i),
                      waits=[t_pv, st["rd_free"][rb]])
            t_hi = COPY(P, "dve", rdh[rb][64:65, 0:256], rd[rb][64:65, 0:256], waits=[t2, mrh_free[rb]])
            t2 = TT(P, "dve", rdl[rb][64:65, 0:256], rd[rb][64:65, 0:256], rdh[rb][64:65, 0:256], ALU.subtract, waits=[t_hi])
            st["rd_free"][rb] = t2
        return rb, t2

    mrh_free = [None, None]

    def stage_E(rb, t_rd):
        MM(P, PS[BCB][:, 0:256], halfs_bf[64:65, 0:128], rdh[rb][64:65, 0:256], True, False,
           waits=[t_rd, bank_free[BCB], t_hb])
        t = MM(P, PS[BCB][:, 0:256], halfs_bf[64:65, 0:128], rdl[rb][64:65, 0:256], False, True, inc=True)
        mrh_free[rb] = t
        return t

    def stage_F(bankP, t_bc, Zt, half, swa=False, t_pv=None):
        wb = st["wb"] % 2
        st["wb"] += 1
        rows = slice(half * 64, half * 64 + 64)
        if swa:
            t4a = COPY(P, "act", otmp[wb][rows, 0:256], PS[bankP][0:64, 0:256], waits=[t_pv, st["o_free"][wb]])
            bank_free[bankP] = t4a
            t3 = TT(P, "dve", wtmp[wb][rows, 0:256], Zt[rows, :], PS[BCB][rows, 0:256], ALU.mult,
                    waits=[t_bc, st["w_free"][wb]])
            bank_free[BCB] = t3
            t4 = TT(P, "pool", Zt[rows, :], otmp[wb][rows, 0:256], wtmp[wb][rows, 0:256], ALU.mult, waits=[t3, t4a])
            st["o_free"][wb] = t4
            st["w_free"][wb] = t4
            return t4
        t3 = TT(P, "dve", wtmp[wb][rows, 0:256], Zt[rows, :], PS[BCB][rows, 0:256], ALU.mult,
                waits=[t_bc, st["w_free"][wb]])
        bank_free[BCB] = t3
        if half == 0:
            t4 = TT(P, "dve", Zt[rows, :], PS[bankP][0:64, 0:256], wtmp[wb][rows, 0:256], ALU.mult, waits=[t3])
        else:
            t4a = COPY(P, "dve", otmp[wb][rows, 0:256], PS[bankP][0:64, 0:256], waits=[st["o_free"][wb]])
            t4 = TT(P, "dve", Zt[rows, :], otmp[wb][rows, 0:256], wtmp[wb][rows, 0:256], ALU.mult, waits=[t3, t4a])
            st["o_free"][wb] = t4
        st["w_free"][wb] = t4
        bank_free[bankP] = t4
        return t4

    KA1 = XT[:, 0:8192]
    w_ka = [t_p2_pe, t_p2_last_evac]
    P.op("pool", lambda e: e.memset(KA1[0:64, :], 0.0), waits=w_ka, inc=False)
    t_c1 = COPY(P, "pool", KA1[64:128, :], KA[64:128, :], waits=w_ka)
    t_kap = P.op("pool", lambda e: e.memset(KA[64:128, :], 0.0), waits=[t_c1])
    P.op("pool", lambda e: e.memset(rd[0][:, :], 1.0), waits=w_ka + [t_qsp[NSLOT - 2], t_qsp[NSLOT - 1]], inc=False)
    t_rdinit = P.op("pool", lambda e: e.memset(rd[1][:, :], 1.0), waits=w_ka)
    KAp = [KA, KA1]
    swa_rh_free = [None, None]
    if stage == 28:
        return _finish()
    units = [(i, m) for i in range(NSLOT) for m in range(4)]
    if stage in (29, 291, 292, 293):
        units = units[:6]
    pt_free = [None, None, None]
    sw = {}
    scnt = 0
    pcnt = 0
    NU = len(units)
    t_bc = None
    for n in range(NU + 2):
        if n < NU:
            i, m = units[n]
            g = m // 2
            pa = (2 * m) % 4
            rows = slice(g * 64, g * 64 + 64)
            tok0 = (2 * i + 1) * 256
            qs = i * 256
            sbk = scnt % 2
            pb = scnt % 3
            scnt += 1
            w0 = [bank_free[2 * sbk], bank_free[2 * sbk + 1], t_h, t_cb, t_p2_last_evac, t_kap]
            t_s = None
            for e_ in range(2):
                h = 2 * m + e_
                off = e_ * 512
                MM(P, PSB[sbk][:, off:off + 256], KAp[g][:, tok0:tok0 + 128], QA[:, pa + e_, qs:qs + 256], True, False,
                   waits=w0 if e_ == 0 else [])
                MM(P, PSB[sbk][:, off:off + 256], antiid[:, :], Htab[:, h * 3 + 0, :], False, True)
                MM(P, PSB[sbk][:, off + 256:off + 384], KAp[g][:, tok0 + 128:tok0 + 256], QA[:, pa + e_, qs + 128:qs + 256], True, False)
                MM(P, PSB[sbk][:, off + 256:off + 384], antiid[:, :], Htab[:, h * 3 + 1, 128:256], False, True)
                MM(P, PSB[sbk][:, off + 384:off + 512], KAp[g][:, tok0 - 128:tok0], QA[:, pa + e_, qs:qs + 128], True, False)
                t_s = MM(P, PSB[sbk][:, off + 384:off + 512], antiid[:, :], Htab[:, h * 3 + 2, 0:128], False, True, inc=(e_ == 1))
            if i == 0:
                ACTF(P, PT[pb][:, 0:384], PSB[sbk][:, 0:384], AF.Exp, scale=0.125, waits=[t_s, pt_free[pb]], inc=False)
                ACTF(P, PT[pb][:, 512:896], PSB[sbk][:, 512:896], AF.Exp, scale=0.125, inc=False)
                ACTF(P, PT[pb][:, 384:512], PSB[sbk][:, 384:512], AF.Exp, bias=firstneg[:, 0:1], scale=0.125, waits=[t_c], inc=False)
                t_e = ACTF(P, PT[pb][:, 896:1024], PSB[sbk][:, 896:1024], AF.Exp, bias=firstneg[:, 0:1], scale=0.125)
            else:
                t_e = ACTF(P, PT[pb][:, :], PSB[sbk][:, :], AF.Exp, scale=0.125, waits=[t_s, pt_free[pb]])
            bank_free[2 * sbk] = t_e
            bank_free[2 * sbk + 1] = t_e
            sw[n] = dict(pb=pb, t_e=t_e, i=i, m=m, g=g, pa=pa)
        if 1 <= n <= NU and stage != 291:
            u = sw[n - 1]
            i, m, g, pb = u["i"], u["m"], u["g"], u["pb"]
            T0 = (2 * i + 1) * 2
            bp = 4 + pcnt % 3
            pcnt += 1
            t_pv = None
            for e_ in range(2):
                c0 = e_ * 256
                o0 = e_ * 512
                MM(P, PS[bp][0:65, c0:c0 + 256], VA[:, T0, g, :], PT[pb][:, o0:o0 + 256], True, False,
                   waits=[u["t_e"], bank_free[bp]] if e_ == 0 else [])
                MM(P, PS[bp][0:65, c0 + 128:c0 + 256], VA[:, T0 + 1, g, :], PT[pb][:, o0 + 256:o0 + 384], False, False)
                t_pv = MM(P, PS[bp][0:65, c0:c0 + 128], VA[:, T0 - 1, g, :], PT[pb][:, o0 + 384:o0 + 512], False, True,
                          inc=(e_ == 1))
            pt_free[pb] = t_pv
            rb = st["rb"] % 2
            st["rb"] += 1
            ACTF(P, rd[rb][64:65, 0:256], PS[bp][64:65, 0:256], AF.Identity, bias=es_sb[64:65, 2 * m:2 * m + 1], scale=1.0,
                 waits=[t_pv, st["rd_free"][rb], t_es, t_rdinit], inc=False)
            t_d1 = ACTF(P, rd[rb][0:1, 0:256], PS[bp][64:65, 256:512], AF.Identity,
                        bias=es_sb[64:65, 2 * m + 1:2 * m + 2], scale=1.0)
            t_rc = P.op("dve", lambda e, o=rd[rb][0:65, 0:256]: e.reciprocal(out=o, in_=o), waits=[t_d1])
            t_mv = COPY(P, "act", rd[rb][64:65, 256:512], rd[rb][0:1, 0:256], waits=[t_rc])
            t_hi = COPY(P, "dve", rdh[rb][64:65, 0:512], rd[rb][64:65, 0:512], waits=[t_mv, swa_rh_free[rb]])
            t_rd = TT(P, "dve", rdl[rb][64:65, 0:512], rd[rb][64:65, 0:512], rdh[rb][64:65, 0:512], ALU.subtract, waits=[t_hi])
            st["rd_free"][rb] = t_rd
            u.update(bp=bp, rb=rb, t_rd=t_rd, t_pv=t_pv)
        if 2 <= n and stage not in (291, 292):
            u = sw[n - 2]
            rb, bp, g, pa, i = u["rb"], u["bp"], u["g"], u["pa"], u["i"]
            rows = slice(g * 64, g * 64 + 64)
            MM(P, PS[BCB][:, 0:512], halfs_bf[64:65, 0:128], rdh[rb][64:65, 0:512], True, False,
               waits=[u["t_rd"], bank_free[BCB], t_hb])
            t_bc = MM(P, PS[BCB][:, 0:512], halfs_bf[64:65, 0:128], rdl[rb][64:65, 0:512], False, True, inc=True)
            swa_rh_free[rb] = t_bc
            if stage == 293:
                bank_free[BCB] = None
                continue
            wb = st["wb"] % 2
            st["wb"] += 1
            Zp = ZAv[rows, pa:pa + 2, i * 256:(i + 1) * 256]
            t4a = COPY(P, "act", otmp[wb][rows, 0:512], PS[bp][0:64, 0:512], waits=[u["t_pv"], st["o_free"][wb]])
            bank_free[bp] = t4a
            t3 = TT(P, "dve", wtmp[wb][rows, 0:512].rearrange("p (a t) -> p a t", a=2), Zp,
                    PS[BCB][rows, 0:512].rearrange("p (a t) -> p a t", a=2), ALU.mult,
                    waits=[t_bc, st["w_free"][wb]])
            bank_free[BCB] = t3
            t4 = TT(P, "pool", Zp, otmp[wb][rows, 0:512].rearrange("p (a t) -> p a t", a=2),
                    wtmp[wb][rows, 0:512].rearrange("p (a t) -> p a t", a=2), ALU.mult, waits=[t3, t4a])
            st["o_free"][wb] = t4
            st["w_free"][wb] = t4
    t_swa_done_pe = t_bc
    t_swa_pool = (P.psem["pool"], P.pcount["pool"])
    if stage in (3, 29, 291, 292, 293):
        return _finish()

    DMA(P, "sp", XT[:, 0:1024].bitcast(F32), pastneg_d.ap(), ds_pn, waits=[t_swa_done_pe])
    t_pn = DMA(P, "sp", XT[:, 1024:2048].bitcast(F32), ownsel_d.ap(), ds_pn, waits=[t_swa_done_pe])
    ds_kb = [P.dsem("ds_kb0"), P.dsem("ds_kb1")]
    ds_vb = [P.dsem("ds_vb0"), P.dsem("ds_vb1")]
    ds_qb = [P.dsem("ds_qb0"), P.dsem("ds_qb1")]
    ds_bi = P.dsem("ds_blkind")
    t_bi = None
    for b in range(2):
        t_bi = DMA(P, "pool", KB[b][64:96, :], blkind_d.ap(), ds_bi, waits=[t_swa_done_pe])
    kb_h = kb_s.ap().rearrange("(h d) t -> h d t", d=64)
    qb_h = qb_s.ap().rearrange("(h d) t -> h d t", d=64)
    head_pe_done = {}
    head_loads = {}
    all_spills = [t_ksp[14], t_ksp[15], t_vsp[14], t_vsp[15], t_qsp[14], t_qsp[15]]

    def moba_loads(h):
        bb = h % 2
        w = [head_pe_done.get(h - 2), t_swa_done_pe] + all_spills
        t3 = DMA(P, "sp", QB[bb][0:64, :], qb_h[h], ds_qb[bb], waits=w)
        t1 = DMA(P, "sp", KB[bb][0:64, :], kb_h[h], ds_kb[bb], waits=w)
        t2 = DMA(P, "sp", VB[bb], vb_s.ap()[h], ds_vb[bb], waits=w)
        head_loads[h] = [t1, t2, t3]

    prep_done = {}

    def moba_prep(h):
        bb = h % 2
        lw = head_loads[h]
        kmh = kmean[0:64, h * 32:(h + 1) * 32]
        bk = 7
        tg = []
        for half in range(2):
            for t in range(16):
                tt = half * 16 + t
                tm = MM(P, PS[bk][:, t * 32:(t + 1) * 32], QB[bb][0:64, tt * 128:(tt + 1) * 128], kmh, True, True,
                        waits=[lw[2], t_kmean, bank_free[bk]] if t == 0 else [], inc=(t == 15))
            g = TT(P, "dve", gm[:, half * 16:(half + 1) * 16, :].rearrange("p (s two) v -> p s two v", two=2),
                   PS[bk][:, :].rearrange("p (s two v) -> p s two v", two=2, v=32),
                   pastneg[:, half * 8:(half + 1) * 8, :].unsqueeze(2).broadcast_to([128, 8, 2, 32]),
                   ALU.add, waits=[tm, t_pn, prep_done.get(h - 1)])
            bank_free[bk] = g
            tg.append(g)
        tmx = None
        for tt in range(32):
            tmx = P.op("dve", lambda e, o=top8[:, tt, :], i_=gm[:, tt, :]: e.max(out=o, in_=i_),
                       waits=tg if tt == 0 else [], inc=(tt == 31))
        t1 = TS(P, "dve", thr[:, 0:32], top8[:, :, 2], -1e8, None, ALU.max, waits=[tmx])
        t2 = TT(P, "dve", sel[:, :, :], gm[:, :, :], thr[:, 0:32].unsqueeze(2).broadcast_to([128, 32, 32]),
                ALU.is_ge, waits=[t1])
        t3 = TT(P, "dve", sel[:, :, :].rearrange("p (s two) v -> p s two v", two=2),
                sel[:, :, :].rearrange("p (s two) v -> p s two v", two=2),
                ownsel[:, :, :].unsqueeze(2).broadcast_to([128, 16, 2, 32]), ALU.max, waits=[t2])
        t4 = TS(P, "dve", nm[:, :, :], sel[:, :, :], -NEGM, NEGM, ALU.mult, ALU.add, waits=[t3])
        prep_t4[h] = t4

    prep_t4 = {}

    def moba_prep_b_step(h, g8):
        bb = h % 2
        bk = 7
        for q4 in range(4):
            tt = g8 * 4 + q4
            tm = MM(P, PS[bk][0:32, q4 * 128:(q4 + 1) * 128], nm[:, tt, :], ident[:, :], True, True,
                    waits=[prep_t4[h], t_cb, bank_free[bk]] if q4 == 0 else [], inc=(q4 == 3))
        tc = COPY(P, "dve", QB[bb][64:96, g8 * 512:(g8 + 1) * 512], PS[bk][0:32, :], waits=[tm])
        bank_free[bk] = tc
        if g8 == 7:
            prep_done[h] = tc

    def moba_prep_b(h):
        for g8 in range(8):
            moba_prep_b_step(h, g8)

    mst = {"u": 0, "pv": 0}

    def moba_attention(h, mid_hook):
        bb = h % 2
        half, p = h % 2, h // 2
        lw = head_loads[h] + [prep_done[h], t_bi, t_h, t_cb]
        ulist = [(i, w) for i in range(NSLOT) for w in range(i + 1)]
        pend = {}
        tails = []
        NUu = len(ulist)
        LAG = 2
        for n in range(NUu + LAG):
            if n < NUu:
                i, w = ulist[n]
                sb = mst["u"] % 2
                pb = mst["u"] % 3
                mst["u"] += 1
                qs = i * 256
                lastw = (w == i)
                first = True
                for vv in range(2):
                    v = 2 * w + vv
                    for a in range(2):
                        kt = v * 256 + a * 128
                        col = (vv * 2 + a) * 256
                        bkk = 2 * sb + vv
                        special = lastw and (vv == 1 or a == 1)
                        fin = (vv == 1 and a == 1)
                        MM(P, PSB[sb][:, col:col + 256], KB[bb][0:96, kt:kt + 128], QB[bb][0:96, qs:qs + 256],
                           True, not special, waits=(lw + [bank_free[2 * sb], bank_free[2 * sb + 1]]) if first else [],
                           inc=(fin and not special))
                        first = False
                        if special:
                            kind = a if vv == 1 else 2
                            MM(P, PSB[sb][:, col:col + 256], antiid[:, :], Htab[:, (8 + h) * 3 + kind, :],
                               False, True, inc=fin)
                t_s = (P.psem["pe"], P.pcount["pe"])
                t_e = ACTF(P, PT[pb][:, :], PSB[sb][:, :], AF.Exp, scale=0.125, waits=[t_s, pt_free[pb]])
                bank_free[2 * sb] = t_e
                bank_free[2 * sb + 1] = t_e
                pend[n] = dict(pb=pb, t_e=t_e)
            if n >= LAG:
                i, w = ulist[n - LAG]
                u = pend.pop(n - LAG)
                if w == 0:
                    mst["bp"] = 4 + mst["pv"] % 3
                    mst["pv"] += 1
                    while any(tl[2] == mst["bp"] for tl in tails):
                        tails.pop(0)[1]()
                bp = mst["bp"]
                lastw = (w == i)
                for vv in range(2):
                    v = 2 * w + vv
                    for a in range(2):
                        col = (vv * 2 + a) * 256
                        fin = (vv == 1 and a == 1)
                        t_pv = MM(P, PS[bp][0:65, 0:256], VB[bb][:, v * 2 + a, :], PT[u["pb"]][:, col:col + 256],
                                  (w == 0 and vv == 0 and a == 0), (lastw and fin),
                                  waits=[u["t_e"], bank_free[bp]] if (vv == 0 and a == 0) else [], inc=fin)
                pt_free[u["pb"]] = t_pv
                if lastw:
                    rb, t_rd = stage_D(bp, t_pv, None)
                    Zt = ZBv[:, p, i * 256:(i + 1) * 256]
                    tails.append([5, (lambda rb=rb, t_rd=t_rd, bp=bp, Zt=Zt: stage_F(bp, stage_E(rb, t_rd), Zt, half)), bp])
                    if mid_hook is not None:
                        mid_hook(i)
            for tl in tails:
                tl[0] -= 1
            while tails and tails[0][0] <= 0:
                tails.pop(0)[1]()
        for tl in tails:
            tl[1]()
        head_pe_done[h] = (P.psem["pe"], P.pcount["pe"])

    moba_loads(0)
    if stage == 31:
        return _finish()
    moba_prep(0)
    moba_prep_b(0)
    if stage == 32:
        return _finish()
    moba_loads(1)
    for h in range(8):
        def hook(i, h=h):
            if h + 1 < 8:
                if i == 4:
                    moba_prep(h + 1)
                if 7 <= i < 15:
                    moba_prep_b_step(h + 1, i - 7)
        use_hook = stage not in (33, 36)
        if not use_hook and h + 1 < 8 and stage == 36:
            moba_prep(h + 1)
            moba_prep_b(h + 1)
        moba_attention(h, hook if use_hook else None)
        if stage in (33, 34):
            return _finish()
        if stage == 36 and h == 1:
            return _finish()
        if stage >= 40 and stage < 48 and h == stage - 40:
            return _finish()
        if h + 2 < 8:
            moba_loads(h + 2)
    t_moba_pe = head_pe_done[7]
    t_moba_dve = (P.psem["dve"], P.pcount["dve"])

    if stage == 35:
        return _finish()

    p3_done = [t_moba_pe, t_moba_dve, t_swa_pool]
    ds_w4 = P.dsem("ds_w4")
    ds_w4_dummy = None
    ds_gb = P.dsem("ds_gb")
    DMA(P, "sp", gam, bass.AP(lng_d, 0, [[0, 128], [1, D]]), ds_gb, waits=p3_done)
    t_gb = DMA(P, "sp", bet, bass.AP(lnb_d, 0, [[0, 128], [1, D]]), ds_gb, waits=p3_done)
    ds_xr = [P.dsem("ds_xr0"), P.dsem("ds_xr1")]
    ds_out = [P.dsem(f"ds_out{k}") for k in range(4)]
    pe_g_done = {}
    t_out_done = None
    t_tmp_free = [None] * 4
    xr_free = [None, None]
    yt_free = [None] * 4
    out_toks = []
    lncnt = 0
    lnm = lnst[:, :].rearrange("p (t k) -> p t k", k=6)
    t_x4 = {}

    def load_x4(tc):
        b = tc % 2
        w = [pe_g_done.get(tc - 2)] + p3_done
        DMA(P, "pool", xt[b][:, :, 0:256], xT_v[:, :, (4 * tc + 1) * 256:(4 * tc + 2) * 256], ds_xt[b], waits=w)
        t_x4[tc] = DMA(P, "pool", xt[b][:, :, 256:512], xT_v[:, :, (4 * tc + 3) * 256:(4 * tc + 4) * 256], ds_xt[b], waits=w)

    pending_norm = []
    load_x4(0)
    ds_wj = [P.dsem(f"ds_wj{j}") for j in range(8)]
    t_wj = []
    for j in range(8):
        DMA(P, "pool", Wg[:, :, j * 128:(j + 1) * 128], wg_d.ap()[j], ds_wj[j], waits=p3_done)
        DMA(P, "pool", Wg[:, :, 1024 + j * 128:1024 + (j + 1) * 128], wg_d.ap()[8 + j], ds_wj[j], waits=p3_done)
        DMA(P, "pool", Woa[:, :, j * 128:(j + 1) * 128], woa_d.ap()[j], ds_wj[j], waits=p3_done)
        t_wj.append(DMA(P, "pool", Wob[:, :, j * 128:(j + 1) * 128], wob_d.ap()[j], ds_wj[j], waits=p3_done))
    t_w4 = DMA(P, "pool", Wo, wo_d.ap().rearrange("(k p) n -> p k n", p=128), ds_w4, waits=p3_done)
    for tc in range(8):
        b = tc % 2
        if tc + 1 < 8:
            load_x4(tc + 1)
        t_x = t_x4[tc]
        tsl = slice(tc * 512, (tc + 1) * 512)
        for j in range(8):
            bA, bB = j % 2, 2 + j % 2
            for kc in range(8):
                t_gA = MM(P, PS[bA][:, :], Wg[:, kc, j * 128:(j + 1) * 128], xt[b][:, kc, :], kc == 0, kc == 7,
                          waits=[t_x, t_wj[j], bank_free[bA]] if kc == 0 else [], inc=(kc == 7))
            for kc in range(8):
                t_gB = MM(P, PS[bB][:, :], Wg[:, kc, 1024 + j * 128:1024 + (j + 1) * 128], xt[b][:, kc, :], kc == 0, kc == 7,
                          waits=[bank_free[bB]] if kc == 0 else [], inc=(kc == 7))
            for k in range(4):
                t_uA = MM(P, PS[4][:, :], Woa[:, k, j * 128:(j + 1) * 128], ZAv[:, k, tsl], k == 0, k == 3,
                          waits=[bank_free[4]] if k == 0 else [], inc=(k == 3))
            for k in range(4):
                t_uB = MM(P, PS[5][:, :], Wob[:, k, j * 128:(j + 1) * 128], ZBv[:, k, tsl], k == 0, k == 3,
                          waits=[bank_free[5]] if k == 0 else [], inc=(k == 3))
            sig = []
            for (bk, tg, ti, col) in ((bA, t_gA, 0, j), (bB, t_gB, 1, 8 + j)):
                t1 = ACTF(P, tmp4[ti], PS[bk][:, :], AF.Tanh, bias=hbg[:, col:col + 1], scale=0.5,
                          waits=[tg, t_tmp_free[ti], t_nb] + p3_done)
                bank_free[bk] = t1
                t3 = ACTF(P, tmp4[ti], tmp4[ti], AF.Identity, bias=half_c[:, 0:1], scale=0.5, waits=[t1])
                sig.append(t3)
            tA = TT(P, "dve", tmp4[2], tmp4[0], PS[4][:, :], ALU.mult, waits=[sig[0], t_uA, t_tmp_free[2]] + p3_done)
            bank_free[4] = tA
            t_tmp_free[0] = tA
            tB = TT(P, "dve", tmp4[3], tmp4[1], PS[5][:, :], ALU.mult, waits=[sig[1], t_uB, t_tmp_free[3]])
            bank_free[5] = tB
            t_tmp_free[1] = tB
            tM = TT(P, "dve", MG[:, j, :], tmp4[2], tmp4[3], ALU.add, waits=[tA, tB, t_out_done])
            t_tmp_free[2] = tM
            t_tmp_free[3] = tM
            if pending_norm and 1 <= j <= 4:
                pending_norm.pop(0)()
        pe_g_done[tc] = t_uB
        t_ag = None
        for tt in range(4):
            r0 = tc * 512 + tt * 128
            xb = lncnt % 2
            lncnt += 1
            t_xr = DMA(P, "sp", xr[xb], xown_d.ap()[r0:r0 + 128, :], ds_xr[xb], waits=[xr_free[xb]] + p3_done)
            t_o = []
            for hf in range(2):
                bk = 6 + hf
                for jj in range(8):
                    t_mm = MM(P, PS[bk][:, :], MG[:, jj, tt * 128:(tt + 1) * 128], Wo[:, jj, hf * 512:(hf + 1) * 512],
                              jj == 0, jj == 7, waits=[tM, t_w4, bank_free[bk]] if jj == 0 else [], inc=(jj == 7))
                t_o.append(t_mm)
            t_out_done = t_o[1]
            ty = None
            for hf in range(2):
                cs = slice(hf * 512, (hf + 1) * 512)
                ty = STT(P, YT[tt][:, cs], xr[xb][:, cs], ALPHA, PS[6 + hf][:, :], ALU.mult, ALU.add,
                         waits=[t_o[hf], t_xr, yt_free[tt]])
                bank_free[6 + hf] = ty
            xr_free[xb] = ty
            P.op("dve", lambda e, o=lnst[:, 0:6], i_=YT[tt][:, 0:512]: e.bn_stats(out=o, in_=i_), waits=[ty, t_ag], inc=False)
            t_s2 = P.op("dve", lambda e, o=lnst[:, 6:12], i_=YT[tt][:, 512:1024]: e.bn_stats(out=o, in_=i_))
            t_ag = P.op("dve", lambda e, o=lnmv[:, 2 * tt:2 * tt + 2], i_=lnst[:, 0:12]: e.bn_aggr(out=o, in_=i_), waits=[t_s2])
        mv = lnmv[:, 0:8].rearrange("p (t k) -> p t k", k=2)
        t_ve = TS(P, "dve", lnmv[:, 8:12], mv[:, :, 1], LN_EPS, None, ALU.add, waits=[t_ag, yt_free[3]])
        t_l1 = ACTF(P, lnmv[:, 8:12], lnmv[:, 8:12], AF.Ln, waits=[t_ve])
        t_l2 = ACTF(P, lnmv[:, 8:12], lnmv[:, 8:12], AF.Exp, scale=-0.5, waits=[t_l1])
        def norm_store(tt, r0=None, t_l2=t_l2, tc=tc):
            r0 = tc * 512 + tt * 128
            t_n1 = STT(P, YT[tt][:, :], YT[tt][:, :], lnmv[:, 2 * tt:2 * tt + 1], gam, ALU.subtract, ALU.mult,
                       waits=[t_l2, t_gb])
            t_n2 = STT(P, YT[tt][:, :], YT[tt][:, :], lnmv[:, 8 + tt:9 + tt], bet, ALU.mult, ALU.add, waits=[t_n1])
            t_st = DMA(P, "sp", out_d.ap()[r0:r0 + 128, :], YT[tt], ds_out[tt], waits=[t_n2])
            yt_free[tt] = t_st
            out_toks.append(t_st)
        pending_norm[:] = [(lambda tt=tt, f=norm_store: f(tt)) for tt in range(4)]
    for f in pending_norm:
        f()
    P.wait("sp", out_toks[-4:])
    P.emit()
    P.close()
    return nc


def _t5_bucket_np(d):
    n = np.maximum(d, 0)
    nf = np.maximum(n, 1).astype(np.float32)
    large = 16 + (np.log(nf / np.float32(16)) / np.float32(math.log(128 / 16)) * np.float32(16)).astype(np.int32)
    large = np.minimum(large, 31)
    return np.where(n < 16, n, large)


def _constants(par):
    i = np.arange(FL)
    d = i - 255
    bkt = _t5_bucket_np(d)
    ohA = np.zeros((32, FL), np.float32)
    ohB = np.zeros((32, FL), np.float32)
    vA = (d >= 0) & (d <= 127)
    vB = d >= 0
    ohA[bkt[vA], i[vA]] = 1.0
    ohB[bkt[vB], i[vB]] += 1.0
    ohB[31, i[vB]] -= 1.0
    mkA = np.where(vA, 0.0, 8.0 * NEGM).astype(np.float32)[None].repeat(8, 0)
    mkB = np.where(vB, 0.0, 8.0 * NEGM).astype(np.float32)[None].repeat(8, 0)
    pastneg = np.full((NSLOT, 32), -1e9, np.float32)
    ownsel = np.zeros((NSLOT, 32), np.float32)
    for s in range(NSLOT):
        lo = 0 if par == 1 else 1
        pastneg[s, lo:2 * s + 1] = 0.0
        ownsel[s, 2 * s + 1] = 1.0
    pastneg = np.ascontiguousarray(np.broadcast_to(pastneg.reshape(1, -1), (128, NSLOT * 32)))
    ownsel = np.ascontiguousarray(np.broadcast_to(ownsel.reshape(1, -1), (128, NSLOT * 32)))
    firstneg = np.full((128, 1), NEGM if par == 0 else 0.0, np.float32)
    ident = np.eye(128, dtype=np.float32)
    antiid = np.ascontiguousarray(ident[::-1])
    blkind = np.zeros((32, SEQ), np.float32)
    for v in range(32):
        blkind[v, v * 256:(v + 1) * 256] = 1.0
    return dict(ohA=ohA, ohB=ohB, mkA=mkA, mkB=mkB, pastneg=pastneg, ownsel=ownsel, firstneg=firstneg,
                ident=ident, antiid=antiid, blkind=blkind)


def _weights(w_in, b_gate, sinks, rel_table, w_out_a, w_out_b, w_out, ln_gamma, ln_beta):
    w = np.asarray(w_in)[0]
    qA, kA, vA, zA = w[:, 0:512], w[:, 512:640], w[:, 640:768], w[:, 768:1280]
    qB, kB, vB, zB = w[:, 1280:1792], w[:, 1792:2304], w[:, 2304:2816], w[:, 2816:3328]
    gA, gB = w[:, 3328:4352], w[:, 4352:5376]

    def pair48(m):
        hm = m.reshape(m.shape[0], 8, 64)
        return np.concatenate([np.concatenate([hm[:, p], hm[:, p + 4]], axis=1) for p in range(4)], axis=1)

    w_kv = np.ascontiguousarray(np.concatenate([kA, kB, vA, vB], axis=1))
    w_own = np.ascontiguousarray(np.concatenate([pair48(qA), pair48(zA), qB, zB], axis=1))
    w_g = np.ascontiguousarray(np.concatenate([gA, gB], axis=1))
    woa = np.asarray(w_out_a)[0].reshape(8, 64, D)
    w_oa = np.ascontiguousarray(np.concatenate([np.concatenate([woa[p], woa[p + 4]], axis=0) for p in range(4)], axis=0))
    w_ob = np.ascontiguousarray(np.asarray(w_out_b)[0])
    w_o = np.ascontiguousarray(np.asarray(w_out)[0])
    bg = np.ascontiguousarray(np.asarray(b_gate)[0].reshape(16, 128).T)
    w_g = np.ascontiguousarray(w_g.reshape(8, 128, 16, 128).transpose(2, 1, 0, 3))
    w_oa = np.ascontiguousarray(w_oa.reshape(4, 128, 8, 128).transpose(2, 1, 0, 3))
    w_ob = np.ascontiguousarray(w_ob.reshape(4, 128, 8, 128).transpose(2, 1, 0, 3))
    return dict(w_kv=w_kv, w_own=w_own, w_g=w_g, w_oa=w_oa, w_ob=w_ob, w_o=w_o, bg=bg,
                rel=np.ascontiguousarray(np.asarray(rel_table), dtype=np.float32),
                sinks=np.ascontiguousarray(np.asarray(sinks)[0].reshape(1, 8)),
                lng=np.ascontiguousarray(np.asarray(ln_gamma)[0].reshape(1, D)),
                lnb=np.ascontiguousarray(np.asarray(ln_beta)[0].reshape(1, D)))


_NC_CACHE = {}


def kernel(x, w_in, b_gate, sinks, rel_table, w_out_a, w_out_b, w_out, ln_gamma, ln_beta):
    x = np.asarray(x, dtype=np.float32)
    wd = _weights(w_in, b_gate, sinks, rel_table, w_out_a, w_out_b, w_out, ln_gamma, ln_beta)
    wd = {k: np.asarray(v, dtype=np.float32) for k, v in wd.items()}
    in_maps = []
    for c in range(NCORES):
        b, par = c // 2, c % 2
        xb = x[b]
        if par == 1:
            virt = xb
        else:
            virt = np.concatenate([np.zeros((256, D), np.float32), xb[:SEQ - 256]], axis=0)
        xT = np.ascontiguousarray(virt.T)
        xown = np.ascontiguousarray(virt.reshape(32, 256, D)[1::2].reshape(NSLOT * 256, D))
        m = dict(xT=xT, xown=xown)
        m.update(wd)
        m.update(_constants(par))
        in_maps.append(m)
    if "nc" not in _NC_CACHE:
        _NC_CACHE["nc"] = build_nc()
    nc = _NC_CACHE["nc"]
    res = run_bass_kernel_spmd(nc, in_maps, core_ids=list(range(NCORES)))
    out = np.empty((4, SEQ, D), np.float32)
    for c in range(NCORES):
        b, par = c // 2, c % 2
        o = np.asarray(res.results[c]["out"]).reshape(NSLOT, 256, D)
        out[b].reshape(32, 256, D)[par::2] = o
    return out
```
